# Optimizing a Trainium2 kernel written in Bass

```python
import jax, jax.numpy as jnp
from jax import lax

D_MODEL = 1024
BATCH = 8
SEQ = 4096
DEPTH = 4

N_HEADS = 16
HEAD_DIM = 128
D_INNER = N_HEADS * HEAD_DIM
Q_BLOCK = 128
N_MIXERS = 2
LN_EPS = 1e-5
DEEPNORM_ALPHA = (2 * DEPTH) ** 0.25
DEEPNORM_BETA = (8 * DEPTH) ** -0.25
FOX_COLS = 4 * D_INNER + N_HEADS
SB_COLS = 4 * D_INNER
N_FOX = (DEPTH + 1) // 2
N_SB = DEPTH // 2

kernel_name = "fox_stickbreaking_interleaved_deepnorm"


def _to_blocks(t):
    b, s = t.shape[:2]
    t = t.reshape((b, s // Q_BLOCK, Q_BLOCK) + t.shape[2:])
    return jnp.moveaxis(t, 1, 0)


def _from_blocks(t):
    nb, b, qb = t.shape[:3]
    return jnp.moveaxis(t, 0, 1).reshape(b, nb * qb, -1)


def forgetting_attention(q, k, v, log_f):
    s = q.shape[1]
    scale = HEAD_DIM ** -0.5
    cum = jnp.cumsum(log_f, axis=1)
    cum_k = jnp.transpose(cum, (0, 2, 1))
    k_pos = jnp.arange(s)

    def one_block(args):
        q_blk, c_blk, blk = args
        q_pos = blk * Q_BLOCK + jnp.arange(Q_BLOCK)
        logits = jnp.einsum('bqhd,bkhd->bhqk', q_blk, k) * scale
        logits = logits + jnp.transpose(c_blk, (0, 2, 1))[..., None] - cum_k[:, :, None, :]
        causal = k_pos[None, :] <= q_pos[:, None]
        logits = jnp.where(causal, logits, -jnp.inf)
        p = jax.nn.softmax(logits, axis=-1)
        return jnp.einsum('bhqk,bkhd->bqhd', p, v)

    nb = s // Q_BLOCK
    out = lax.map(one_block, (_to_blocks(q), _to_blocks(cum), jnp.arange(nb)))
    return _from_blocks(out)


def stick_breaking_attention(q, k, v):
    s = q.shape[1]
    scale = HEAD_DIM ** -0.5
    k_pos = jnp.arange(s)

    def one_block(args):
        q_blk, blk = args
        q_pos = blk * Q_BLOCK + jnp.arange(Q_BLOCK)
        z = jnp.einsum('bqhd,bkhd->bhqk', q_blk, k) * scale
        strict = k_pos[None, :] < q_pos[:, None]
        log_one_minus = jnp.where(strict, jax.nn.log_sigmoid(-z), 0.0)
        remain = lax.cumsum(log_one_minus, axis=3, reverse=True) - log_one_minus
        weights = jnp.where(strict, jnp.exp(jax.nn.log_sigmoid(z) + remain), 0.0)
        return jnp.einsum('bhqk,bkhd->bqhd', weights, v)

    nb = s // Q_BLOCK
    out = lax.map(one_block, (_to_blocks(q), jnp.arange(nb)))
    return _from_blocks(out)


def _layer_norm(x, g, b):
    xf = x.astype(jnp.float32)
    mu = jnp.mean(xf, axis=-1, keepdims=True)
    var = jnp.mean(jnp.square(xf - mu), axis=-1, keepdims=True)
    y = (xf - mu) * lax.rsqrt(var + LN_EPS) * g.astype(jnp.float32) + b.astype(jnp.float32)
    return y.astype(x.dtype)


def setup_inputs(seed: int = 0) -> dict:
    key = jax.random.key(seed)
    ks = jax.random.split(key, 8)
    x = jax.random.normal(ks[0], (BATCH, SEQ, D_MODEL), jnp.float32)

    fox_scale = jnp.ones((FOX_COLS,), jnp.float32).at[2 * D_INNER:3 * D_INNER].set(DEEPNORM_BETA)
    sb_scale = jnp.ones((SB_COLS,), jnp.float32).at[2 * D_INNER:3 * D_INNER].set(DEEPNORM_BETA)
    fox_w_in = jax.random.normal(ks[1], (N_FOX, D_MODEL, FOX_COLS), jnp.float32) * (D_MODEL ** -0.5) * fox_scale
    sb_w_in = jax.random.normal(ks[2], (N_SB, D_MODEL, SB_COLS), jnp.float32) * (D_MODEL ** -0.5) * sb_scale
    fox_b_f = jax.random.uniform(ks[3], (N_FOX, N_HEADS), jnp.float32, minval=1.0, maxval=4.0)
    out_std = (D_INNER ** -0.5) * DEEPNORM_BETA
    fox_w_out = jax.random.normal(ks[4], (N_FOX, D_INNER, D_MODEL), jnp.float32) * out_std
    sb_w_out = jax.random.normal(ks[5], (N_SB, D_INNER, D_MODEL), jnp.float32) * out_std
    ln_g = 1.0 + 0.02 * jax.random.normal(ks[6], (DEPTH, D_MODEL), jnp.float32)
    ln_b = 0.02 * jax.random.normal(ks[7], (DEPTH, D_MODEL), jnp.float32)
    return {"x": x, "fox_w_in": fox_w_in, "fox_b_f": fox_b_f, "fox_w_out": fox_w_out,
            "sb_w_in": sb_w_in, "sb_w_out": sb_w_out, "ln_g": ln_g, "ln_b": ln_b}


def reference(x, fox_w_in, fox_b_f, fox_w_out, sb_w_in, sb_w_out, ln_g, ln_b):
    b, s, _ = x.shape
    for layer in range(DEPTH):
        slot = layer // N_MIXERS
        use_fox = (layer % N_MIXERS) == 0
        w_in = fox_w_in[slot] if use_fox else sb_w_in[slot]
        w_out = fox_w_out[slot] if use_fox else sb_w_out[slot]

        h = x @ w_in
        q = h[..., 0 * D_INNER:1 * D_INNER].reshape(b, s, N_HEADS, HEAD_DIM).astype(jnp.float32)
        k = h[..., 1 * D_INNER:2 * D_INNER].reshape(b, s, N_HEADS, HEAD_DIM).astype(jnp.float32)
        v = h[..., 2 * D_INNER:3 * D_INNER].reshape(b, s, N_HEADS, HEAD_DIM).astype(jnp.float32)
        z = h[..., 3 * D_INNER:4 * D_INNER]

        if use_fox:
            f_logit = h[..., 4 * D_INNER:].astype(jnp.float32)
            log_f = jax.nn.log_sigmoid(f_logit + fox_b_f[slot].astype(jnp.float32))
            o = forgetting_attention(q, k, v, log_f)
        else:
            o = stick_breaking_attention(q, k, v)

        y = (o.astype(x.dtype) * jax.nn.silu(z)) @ w_out
        x = _layer_norm(DEEPNORM_ALPHA * x + y, ln_g[layer], ln_b[layer])
    return x
```

```python
import numpy as np
import concourse.bass as bass
import concourse.mybir as mybir
from concourse.bass_utils import run_bass_kernel_spmd
from contextlib import ExitStack

F32 = mybir.dt.float32
BF16 = mybir.dt.bfloat16
AF = mybir.ActivationFunctionType
ALU = mybir.AluOpType

LN_EPS = 1e-5
MASKV = -30000.0


class Cfg:
    def __init__(self, S=4096, D=1024, H=16, DEPTH=4):
        self.S, self.D, self.H, self.DEPTH = S, D, H, DEPTH
        self.dh = 128
        self.KD = D // 128
        self.NT = S // 128
        self.NQ = S // 512
        self.DI = H * 128
        self.alpha = (2 * DEPTH) ** 0.25
        self.scale = 128 ** -0.5
        self.NFOX = (DEPTH + 1) // 2
        self.NSB = DEPTH // 2
        self.mixers = ['fox' if l % 2 == 0 else 'sb' for l in range(DEPTH)]
        self.c_ident = 0
        self.c_ntri = 128
        self.c_ctri = 256
        self.c_ones = 384
        self.c_maskf = 512
        self.c_masks = 512 + 2048
        self.c_sel = 512 + 4096
        self.NCB = 512 + 4096 + H * 128


def make_consts(cfg):
    c = np.zeros((128, cfg.NCB), np.float32)
    j = np.arange(128)[:, None]
    s = np.arange(128)[None, :]
    c[:, cfg.c_ident:cfg.c_ident + 128] = (j == s)
    c[:, cfg.c_ntri:cfg.c_ntri + 128] = -(j >= s).astype(np.float32)
    c[:, cfg.c_ctri:cfg.c_ctri + 128] = -(j < s).astype(np.float32)
    c[:, cfg.c_ones:cfg.c_ones + 128] = 1.0
    tq = np.arange(512)[None, :]
    for i in range(4):
        key = 128 * i + np.arange(128)[:, None]
        c[:, cfg.c_maskf + i * 512: cfg.c_maskf + (i + 1) * 512] = np.where(key <= tq, 0.0, MASKV)
        c[:, cfg.c_masks + i * 512: cfg.c_masks + (i + 1) * 512] = np.where(key < tq, 0.0, MASKV)
    for h in range(cfg.H):
        c[h, cfg.c_sel + h * 128: cfg.c_sel + (h + 1) * 128] = 1.0
    cf = np.zeros((128, 64), np.float32)
    cf[:, 0] = 1.0
    cf[:, 1] = LN_EPS
    cf[:, 2] = 0.0
    cf[:32, 8:40] = np.eye(32, dtype=np.float32)
    return c, cf


def layout_weights(cfg, w_in_layers, w_out_layers):
    KD, H, DI, D = cfg.KD, cfg.H, cfg.DI, cfg.D
    wq = np.empty((cfg.DEPTH, H, 128, 4 * KD * 128), np.float32)
    wo = np.empty((cfg.DEPTH, 128, H * D), np.float32)
    for l in range(cfg.DEPTH):
        w = w_in_layers[l][:, :4 * DI].reshape(KD, 128, 4, H, 128)
        wq[l] = np.transpose(w, (3, 1, 2, 0, 4)).reshape(H, 128, 4 * KD * 128)
        o = w_out_layers[l].reshape(H, 128, D)
        wo[l] = np.transpose(o, (1, 0, 2)).reshape(128, H * D)
    return wq, wo


def layout_wf(cfg, fox_w_in):
    KD, H, DI = cfg.KD, cfg.H, cfg.DI
    n = fox_w_in.shape[0]
    wf = np.empty((n, 128, KD * H), np.float32)
    for s in range(n):
        w = fox_w_in[s][:, 4 * DI:4 * DI + H].reshape(KD, 128, H)
        wf[s] = np.transpose(w, (1, 0, 2)).reshape(128, KD * H)
    return wf


ENGS = ("sync", "tensor", "vector", "scalar", "gpsimd")


class Sem:
    def __init__(self, h, name):
        self.h = h
        self.v = 0
        self.name = name


class Rec:
    def __init__(self, nc, stack):
        self.nc = nc
        self.stack = stack
        self.q = {e: [] for e in ENGS}
        self.waited = {}
        self.nsem = 0

    def sem(self, name):
        h = self.stack.enter_context(self.nc.semaphore(name))
        self.nsem += 1
        return Sem(h, name)

    def op(self, eng, fn, waits=(), inc=None, incv=None):
        mx = {}
        for t in waits:
            if t is None:
                continue
            sem, val = t
            if val <= 0:
                continue
            if id(sem) not in mx or mx[id(sem)][1] < val:
                mx[id(sem)] = (sem, val)
        ws = []
        for sem, val in mx.values():
            key = (eng, id(sem))
            if self.waited.get(key, 0) >= val:
                continue
            self.waited[key] = val
            ws.append((sem, val))
        tick = None
        if inc is not None:
            if incv is None:
                incv = 1
            inc.v += incv
            tick = (inc, inc.v)
        self.q[eng].append((ws, fn, inc, incv))
        return tick

    def replay(self, eng, e):
        for ws, fn, inc, incv in self.q[eng]:
            for sem, val in ws:
                e.wait_ge(sem.h, val)
            if fn is None:
                continue
            ins = fn(e)
            if inc is not None:
                ins.then_inc(inc.h, incv)


def build_program(cfg):
    S, D, H, DEPTH = cfg.S, cfg.D, cfg.H, cfg.DEPTH
    KD, NT, NQ, DI = cfg.KD, cfg.NT, cfg.NQ, cfg.DI
    nc = bass.Bass("TRN2", target_bir_lowering=False)

    x_d = nc.dram_tensor("x", [S, D], F32, kind="ExternalInput").ap()
    wq_d = nc.dram_tensor("wq", [DEPTH, H, 128, 4 * KD * 128], F32, kind="ExternalInput").ap()
    wo_d = nc.dram_tensor("wo", [DEPTH, 128, H * D], F32, kind="ExternalInput").ap()
    wf_d = nc.dram_tensor("wf", [cfg.NFOX, 128, KD * H], F32, kind="ExternalInput").ap()
    bf_d = nc.dram_tensor("bf", [cfg.NFOX, H, 1], F32, kind="ExternalInput").ap()
    lng_d = nc.dram_tensor("lng", [DEPTH, 128, D], F32, kind="ExternalInput").ap()
    lnb_d = nc.dram_tensor("lnb", [DEPTH, 128, D], F32, kind="ExternalInput").ap()
    cb_d = nc.dram_tensor("cb", [128, cfg.NCB], F32, kind="ExternalInput").ap()
    cf_d = nc.dram_tensor("cf", [128, 64], F32, kind="ExternalInput").ap()
    out_d = nc.dram_tensor("out", [S, D], F32, kind="ExternalOutput").ap()
    xres_d = nc.dram_tensor("xres", [S, D], F32, kind="Internal").ap()
    gscr_d = nc.dram_tensor("gscr", [H, 128, S], BF16,
                            kind="ExternalOutput" if getattr(cfg, "debug", False) else "Internal").ap()

    DBG = getattr(cfg, "debug", False)
    if DBG:
        dbgf_d = nc.dram_tensor("dbgf", [8, 2, 128, 512], F32, kind="ExternalOutput").ap()
        dbgb_d = nc.dram_tensor("dbgb", [8, 2, 128, 512], BF16, kind="ExternalOutput").ap()
    with ExitStack() as stack:
        def sb(name, shape, dt):
            return stack.enter_context(nc.sbuf_tensor(name, shape, dt))

        def ps(name, shape, dt):
            return stack.enter_context(nc.psum_tensor(name, shape, dt))

        R = Rec(nc, stack)

        xT = sb("xT", [128, KD, S], BF16)
        cbf = sb("cbf", [128, cfg.NCB], BF16)
        cf = sb("cf32", [128, 64], F32)
        wstage = sb("wstage", [128, 4 * KD * 128], F32)
        whbf = [sb(f"whbf{i}", [128, 4, KD, 128], BF16) for i in range(2)]
        ARENA = 91136
        arena = sb("arena", [128, ARENA // 4], F32)

        class Carver:
            def __init__(self):
                self.off = 0

            def take(self, nbytes, dt, shape):
                assert self.off % 4 == 0
                nb = (nbytes + 31) // 32 * 32
                a = arena[:, self.off // 4:(self.off + nb) // 4]
                self.off += nb
                assert self.off <= ARENA, (self.off, ARENA)
                if dt == BF16:
                    a = a.bitcast(BF16)[:, :nbytes // 2]
                else:
                    a = a[:, :nbytes // 4]
                return a

        ca = Carver()
        psets = []
        set0 = [ca.take(S * 2, BF16, None) for _ in range(4)]
        psets.append(set0)
        gT = [ca.take(S * 2, BF16, None)]
        NSL = getattr(cfg, 'NSL', 3)
        NES = 2
        e_sl = [ca.take(2048, F32, None) for _ in range(NES)]
        er_sl = [ca.take(2048, F32, None) for _ in range(NSL)]
        sp_sl = [ca.take(1024, BF16, None) for _ in range(NSL)]
        a_sl = [ca.take(1024, BF16, None) for _ in range(NSL)]
        set1_off = ca.off
        set1 = [ca.take(S * 2, BF16, None) for _ in range(4)]
        psets.append(set1)
        attn_end_sb = ca.off
        ca.off = set1_off
        rinv = ca.take(2048, F32, None)
        ttmp = ca.take(2048, F32, None)
        Gall = ca.take(S * 2, BF16, None)
        negc_tm = ca.take(NT * H * 4, F32, None)
        negb = ca.take(32, F32, None)
        bfcol = ca.take(32, F32, None)
        wfst = ca.take(KD * H * 4, F32, None)
        wfbf = ca.take(KD * H * 2, BF16, None)
        assert ca.off <= attn_end_sb
        ca.off = attn_end_sb
        attn_end = ca.off
        cfx = Carver()
        e_fm = cfx.take(S * 4, F32, None)
        negc_fm = cfx.take(S * 4, F32, None)
        assert cfx.off <= 4 * S * 2
        cst = Carver()
        cstage = cst.take(cfg.NCB * 4, F32, None)
        assert cst.off <= ARENA
        cc = Carver()
        wobf = cc.take(H * D * 2, BF16, None)
        wost = [cc.take(D * 4, F32, None) for _ in range(2)]
        xblk = [cc.take(D * 4, F32, None) for _ in range(2)]
        rbuf = [cc.take(D * 4, F32, None) for _ in range(2)]
        xnew = [cc.take(D * 4, F32, None) for _ in range(2)]
        xnb = [cc.take(D * 2, BF16, None) for _ in range(2)]
        gblk = [cc.take(H * 128 * 2, BF16, None) for _ in range(2)]
        gB = cc.take(D * 4, F32, None)
        bB = cc.take(D * 4, F32, None)
        stats = cc.take(2 * (D // 512 if D >= 512 else 1) * 6 * 4 + 64, F32, None)

        banks = [ps(f"pb{i}", [128, 512], F32) for i in range(6)]
        tb_bf = [ps(f"ptb{i}", [128, 1024], BF16) for i in range(2)]

        one_col = cf[:, 0:1]
        eps_col = cf[:, 1:2]
        ident32 = cf[0:32, 8:40]
        ident_bf = cbf[:, cfg.c_ident:cfg.c_ident + 128]
        ntri_bf = cbf[:, cfg.c_ntri:cfg.c_ntri + 128]
        ctri_bf = cbf[:, cfg.c_ctri:cfg.c_ctri + 128]
        ones_bf = cbf[:, cfg.c_ones:cfg.c_ones + 128]

        def maskf_bf(i):
            return cbf[:, cfg.c_maskf + i * 512: cfg.c_maskf + (i + 1) * 512]

        def masks_bf(i):
            return cbf[:, cfg.c_masks + i * 512: cfg.c_masks + (i + 1) * 512]

        def sel_bf(h):
            return cbf[:, cfg.c_sel + h * 128: cfg.c_sel + (h + 1) * 128]

        s_pe = R.sem("s_pe")
        s_act = R.sem("s_act")
        s_dve = R.sem("s_dve")
        s_pool = R.sem("s_pool")
        s_dw = R.sem("s_dw")
        s_cst = R.sem("s_cst")
        s_cf = R.sem("s_cf")
        s_wf = R.sem("s_wf")
        s_bf = R.sem("s_bf")
        s_lng = R.sem("s_lng")
        s_lnb = R.sem("s_lnb")
        s_dx = [R.sem(f"s_dx{i}") for i in range(2)]
        s_dg = [R.sem(f"s_dg{i}") for i in range(2)]
        s_dwo = [R.sem(f"s_dwo{i}") for i in range(2)]
        s_gst = [R.sem(f"s_gst{i}") for i in range(2)]
        s_out = [R.sem(f"s_out{i}") for i in range(2)]
        s_dbg = R.sem("s_dbg")

        def PE(fn, waits=(), m=False):
            return R.op("tensor", fn, waits, s_pe if m else None)

        def ACT(fn, waits=(), m=True):
            return R.op("scalar", fn, waits, s_act if m else None)

        def DVE(fn, waits=(), m=True):
            return R.op("vector", fn, waits, s_dve if m else None)

        def POOL(fn, waits=(), m=True):
            return R.op("gpsimd", fn, waits, s_pool if m else None)

        def DMA(fn, sem, waits=()):
            return R.op("sync", fn, waits, sem, 16)

        st = {
            "bank_free": [None] * 6,
            "tb_free": [None, None],
            "xT_ready": [None] * NT,
            "wstage_free": None,
            "whbf_free": [None, None],
            "last": [],
        }

        t_cst = DMA(lambda e: e.dma_start(out=cstage, in_=cb_d[:, :]), s_cst)
        t_cf = DMA(lambda e: e.dma_start(out=cf[:], in_=cf_d[:, :]), s_cf)
        half = cfg.NCB // 2
        t_c1 = DVE(lambda e: e.tensor_copy(out=cbf[:, :half], in_=cstage[:, :half]), [t_cst])
        t_c2 = POOL(lambda e: e.tensor_copy(out=cbf[:, half:], in_=cstage[:, half:]), [t_cst])
        consts_ready = [t_c1, t_c2, t_cf]

        def emit_xT_block(tb, src_f32, src_ticket, castbuf, cast_free_ticket, extra_waits=()):
            i = tb % 2
            t_cast = ACT(lambda e: e.activation(out=castbuf, in_=src_f32, func=AF.Copy),
                         [src_ticket, cast_free_ticket] + list(extra_waits))
            tkt = None
            for kc in range(KD):
                last = kc == KD - 1
                tkt = PE(lambda e, kc=kc: e.transpose(out=tb_bf[i][:, kc * 128:(kc + 1) * 128],
                                                      in_=castbuf[:, kc * 128:(kc + 1) * 128],
                                                      identity=ident_bf),
                         [t_cast, st["tb_free"][i]] + consts_ready, m=last)
            dst = xT[:, :, tb * 128:(tb + 1) * 128]
            src = tb_bf[i][:, :KD * 128].rearrange("p (k c) -> p k c", c=128)
            t_cp = DVE(lambda e: e.tensor_copy(out=dst, in_=src), [tkt])
            st["tb_free"][i] = t_cp
            st["xT_ready"][tb] = t_cp
            return tkt, t_cp

        xb_free = [None, None]
        xnb_free = [None, None]
        for tb in range(NT):
            i = tb % 2
            t_ld = DMA(lambda e, tb=tb, i=i: e.dma_start(out=xblk[i], in_=x_d[tb * 128:(tb + 1) * 128, :]),
                       s_dx[i], [xb_free[i], t_c1, t_c2])
            t_pe, t_cp = emit_xT_block(tb, xblk[i], t_ld, xnb[i], xnb_free[i])
            xb_free[i] = t_pe
            xnb_free[i] = t_pe
        st["last"] = [st["xT_ready"][NT - 1], st["xT_ready"][NT - 2]]
        all_xT = list(st["xT_ready"])

        wl = {"dma": {}, "cast": {}}

        def emit_wload(l, h, extra_waits=()):
            t = DMA(lambda e: e.dma_start(out=wstage[:], in_=wq_d[l, h]), s_dw,
                    [st["wstage_free"]] + list(extra_waits))
            wl["dma"][(l, h)] = t

        def emit_wcast(l, h, pool_only=False):
            slot = (l * H + h) % 2
            src = wstage[:].rearrange("p (t k c) -> p t k c", t=4, k=KD)
            w = [wl["dma"][(l, h)], st["whbf_free"][slot]]
            if pool_only:
                t1 = POOL(lambda e: e.tensor_copy(out=whbf[slot][:, 0:2], in_=src[:, 0:2]), w)
                t2 = POOL(lambda e: e.tensor_copy(out=whbf[slot][:, 2:4], in_=src[:, 2:4]), w)
            else:
                t1 = POOL(lambda e: e.tensor_copy(out=whbf[slot][:, 0:2], in_=src[:, 0:2]), w)
                t2 = DVE(lambda e: e.tensor_copy(out=whbf[slot][:, 2:4], in_=src[:, 2:4]), w)
            st["wstage_free"] = None
            wl["cast"][(l, h)] = [t1, t2]
            return [t1, t2]

        rot = {"i": 0}

        def next_sbank():
            b = rot["i"] % 3
            rot["i"] += 1
            return b

        def emit_fox_pre(l, phase_waits):
            slot = l // 2
            t_wf = DMA(lambda e: e.dma_start(out=wfst, in_=wf_d[slot]), s_wf, phase_waits)
            t_bf = DMA(lambda e: e.dma_start(out=bfcol[0:H, 0:1], in_=bf_d[slot]), s_bf, phase_waits)
            t_wfc = DVE(lambda e: e.tensor_copy(out=wfbf, in_=wfst), [t_wf] + phase_waits)
            t_nb = DVE(lambda e: e.tensor_scalar(out=negb[0:H, 0:1], in0=bfcol[0:H, 0:1], scalar1=-1.0,
                                                 scalar2=None, op0=ALU.mult), [t_bf])
            wfv = wfbf.rearrange("p (k h) -> p k h", h=H)
            t_e = None
            for t in range(NQ):
                b = next_sbank()
                tk = None
                for kc in range(KD):
                    tk = PE(lambda e, kc=kc, t=t, b=b: e.matmul(
                        banks[b][0:H, :], lhsT=wfv[:, kc, :], rhs=xT[:, kc, t * 512:(t + 1) * 512],
                        start=(kc == 0), stop=(kc == KD - 1)),
                        [t_wfc, st["bank_free"][b]] + all_xT + phase_waits, m=(kc == KD - 1))
                t_e = ACT(lambda e, t=t, b=b: e.activation(
                    out=e_fm[0:H, t * 512:(t + 1) * 512], in_=banks[b][0:H, :], func=AF.Exp,
                    bias=negb[0:H, 0:1], scale=-1.0), [tk, t_nb] + phase_waits)
                st["bank_free"][b] = t_e
            t_sp = ACT(lambda e: e.activation(out=e_fm[0:H, :], in_=e_fm[0:H, :], func=AF.Ln,
                                              bias=1.0, scale=1.0), [t_e])
            t_sc = DVE(lambda e: e.tensor_tensor_scan(out=negc_fm[0:H, :], data0=e_fm[0:H, :], data1=e_fm[0:H, :],
                                                      initial=0.0, op0=ALU.add, op1=ALU.max), [t_sp])
            t_z = POOL(lambda e: e.memset(Gall, 0.0), phase_waits)
            t_g = DVE(lambda e: e.tensor_scalar(out=Gall[0:H, :], in0=negc_fm[0:H, :], scalar1=-1.0,
                                                scalar2=None, op0=ALU.mult), [t_sc, t_z])
            tk = None
            for blk in range(NT):
                tk = PE(lambda e, blk=blk: e.transpose(out=banks[3][:, blk * H:(blk + 1) * H],
                                                       in_=negc_fm[0:H, blk * 128:(blk + 1) * 128],
                                                       identity=ident32[0:H, 0:H]),
                        [t_sc, st["bank_free"][3]] + consts_ready, m=(blk == NT - 1))
            t_nc = DVE(lambda e: e.tensor_copy(out=negc_tm, in_=banks[3][:, :NT * H]), [tk])
            st["bank_free"][3] = t_nc
            return [t_g, t_nc]

        def proj_gen(l, h, pset, phase_waits, ovl, outd):
            slot = (l * H + h) % 2
            W = whbf[slot]
            wc = wl["cast"][(l, h)]
            qT_, kT_, szT_, vsb_ = psets[pset]
            base_w = wc + all_xT + phase_waits
            res = {}

            def getbank():
                return 5 if ovl else next_sbank()

            for which in (0, 1, 3):
                for t in range(NQ):
                    b = getbank()
                    tk = None
                    for kc in range(KD):
                        tk = PE(lambda e, kc=kc, t=t, b=b, which=which: e.matmul(
                            banks[b][:, :], lhsT=W[:, which, kc, :], rhs=xT[:, kc, t * 512:(t + 1) * 512],
                            start=(kc == 0), stop=(kc == KD - 1)),
                            base_w + [st["bank_free"][b]], m=(kc == KD - 1))
                        if ovl and kc == KD // 2 - 1:
                            yield
                    if ovl:
                        yield
                    if which == 0:
                        ev = DVE(lambda e, t=t, b=b: e.tensor_scalar(
                            out=qT_[:, t * 512:(t + 1) * 512], in0=banks[b][:, :], scalar1=cfg.scale,
                            scalar2=None, op0=ALU.mult), [tk] + phase_waits)
                        res["q"] = ev
                    elif which == 1:
                        if ovl:
                            ev = DVE(lambda e, t=t, b=b: e.tensor_copy(
                                out=kT_[:, t * 512:(t + 1) * 512], in_=banks[b][:, :]), [tk] + phase_waits)
                            res["k_dve"] = ev
                        else:
                            ev = ACT(lambda e, t=t, b=b: e.activation(
                                out=kT_[:, t * 512:(t + 1) * 512], in_=banks[b][:, :], func=AF.Copy),
                                [tk] + phase_waits)
                            res["k"] = ev
                    else:
                        sl = t % NES
                        t1 = ACT(lambda e, b=b, sl=sl: e.activation(out=e_sl[sl], in_=banks[b][:, :], func=AF.Exp,
                                                                    scale=-1.0), [tk, eslot_free[sl]] + phase_waits)
                        if ovl:
                            yield
                        t2 = ACT(lambda e, sl=sl: e.activation(out=e_sl[sl], in_=e_sl[sl], func=AF.Ln,
                                                               bias=1.0, scale=1.0), [t1])
                        if ovl:
                            yield
                        t3 = ACT(lambda e, sl=sl: e.activation(out=e_sl[sl], in_=e_sl[sl], func=AF.Exp, scale=-1.0),
                                 [t2])
                        if ovl:
                            yield
                        ev = DVE(lambda e, t=t, b=b, sl=sl: e.tensor_tensor(
                            out=szT_[:, t * 512:(t + 1) * 512], in0=banks[b][:, :], in1=e_sl[sl], op=ALU.mult),
                            [t3] + phase_waits)
                        eslot_free[sl] = ev
                        res["z"] = ev
                    st["bank_free"][b] = ev
                    yield
            tk = None
            for g4 in range(NT // 4):
                b = getbank()
                for i4 in range(4):
                    blk = g4 * 4 + i4
                    for kc in range(KD):
                        last = (i4 == 3 and kc == KD - 1)
                        tk = PE(lambda e, kc=kc, blk=blk, b=b, i4=i4: e.matmul(
                            banks[b][:, i4 * 128:(i4 + 1) * 128], lhsT=xT[:, kc, blk * 128:(blk + 1) * 128],
                            rhs=W[:, 2, kc, :], start=(kc == 0), stop=(kc == KD - 1)),
                            base_w + [st["bank_free"][b]], m=last)
                    if ovl and i4 == 1:
                        yield
                if ovl:
                    yield
                if ovl or g4 % 2 == 0:
                    ev = DVE(lambda e, g4=g4, b=b: e.tensor_copy(
                        out=vsb_[:, g4 * 512:(g4 + 1) * 512], in_=banks[b][:, :]), [tk] + phase_waits)
                    res["v_dve"] = ev
                else:
                    ev = ACT(lambda e, g4=g4, b=b: e.activation(
                        out=vsb_[:, g4 * 512:(g4 + 1) * 512], in_=banks[b][:, :], func=AF.Copy),
                        [tk] + phase_waits)
                    res["v_act"] = ev
                st["bank_free"][b] = ev
                yield
            st["whbf_free"][slot] = tk
            outd["ready"] = [res.get(k) for k in ("q", "k", "k_dve", "z", "v_dve", "v_act")]

        gcount = {"n": 0}
        osel = {"n": 0}
        o_free = [None, None]
        g_free = [None, None]
        aslot_free = [None] * NSL
        eslot_free = [None] * NES
        erslot_free = [None] * NSL
        spslot_free = [None] * NSL

        def emit_attn_fox(l, h, ready, fox_ready, phase_waits, pset=0):
            gs = 0
            qT, kT, szT, vsb = psets[pset]
            base = ready + fox_ready + consts_ready + phase_waits
            rsb = banks[3]
            steps = []
            for j in range(NQ):
                for b_ in range(4 * j + 4):
                    steps.append((j, b_))
            n = len(steps)
            oinfo = {}
            for j in range(NQ):
                oinfo[j] = osel["n"] % 2
                osel["n"] += 1
            zb = [None] * n
            t_qk = [None] * n
            t_ex = [None] * n
            t_g_last = [None]

            def crange(j, b_):
                i_ = b_ - 4 * j
                return (128 * i_ if i_ > 0 else 0)

            def qk(i):
                j, b_ = steps[i]
                sbk = next_sbank()
                zb[i] = sbk
                band = b_ >= 4 * j
                c0 = crange(j, b_)
                PE(lambda e: e.matmul(banks[sbk][:, c0:512], lhsT=kT[:, b_ * 128:(b_ + 1) * 128],
                                      rhs=qT[:, j * 512 + c0:(j + 1) * 512], start=True, stop=False),
                   base + [st["bank_free"][sbk]])
                tk = PE(lambda e: e.matmul(banks[sbk][:, c0:512], lhsT=sel_bf(h),
                                           rhs=Gall[:, j * 512 + c0:(j + 1) * 512],
                                           start=False, stop=(not band), skip_group_check=True), base, m=(not band))
                if band:
                    tk = PE(lambda e: e.matmul(banks[sbk][:, c0:c0 + 128], lhsT=ident_bf,
                                               rhs=maskf_bf(b_ - 4 * j)[:, c0:c0 + 128],
                                               start=False, stop=True, skip_group_check=True), base, m=True)
                t_qk[i] = tk

            def ex(i):
                j, b_ = steps[i]
                sl = i % NSL
                sbk = zb[i]
                c0 = crange(j, b_)
                t_ex[i] = ACT(lambda e: e.activation(out=a_sl[sl][:, c0:512], in_=banks[sbk][:, c0:512], func=AF.Exp,
                                                     bias=negc_tm[:, b_ * H + h: b_ * H + h + 1], scale=1.0),
                              [t_qk[i], aslot_free[sl]] + base)
                st["bank_free"][sbk] = t_ex[i]

            def pv(i):
                j, b_ = steps[i]
                sl = i % NSL
                oi = oinfo[j]
                ob = banks[4 + oi]
                first = (b_ == 0)
                lastb = (b_ == 4 * j + 3)
                c0 = crange(j, b_)
                PE(lambda e: e.matmul(ob[:, c0:512], lhsT=vsb[:, b_ * 128:(b_ + 1) * 128], rhs=a_sl[sl][:, c0:512],
                                      start=first, stop=lastb, skip_group_check=True),
                   [t_ex[i], o_free[oi] if first else None] + base)
                tk = PE(lambda e: e.matmul(rsb[:, c0:512], lhsT=ones_bf, rhs=a_sl[sl][:, c0:512], start=first,
                                           stop=lastb, skip_group_check=True),
                        [t_ex[i], st["bank_free"][3] if first else None], m=True)
                aslot_free[sl] = tk
                if lastb:
                    t_c = DVE(lambda e: e.tensor_copy(out=rinv, in_=rsb[:, :]), [tk])
                    st["bank_free"][3] = t_c
                    t_r = DVE(lambda e: e.reciprocal(out=rinv, in_=rinv), [t_c])
                    t_t = DVE(lambda e: e.tensor_tensor(out=ttmp, in0=ob[:, :], in1=rinv, op=ALU.mult), [t_r])
                    o_free[oi] = t_t
                    t_g_last[0] = DVE(lambda e: e.tensor_tensor(
                        out=gT[gs][:, j * 512:(j + 1) * 512], in0=ttmp, in1=szT[:, j * 512:(j + 1) * 512],
                        op=ALU.mult), [t_t, g_free[gs]] + base)

            qk(0)
            if n > 1:
                qk(1)
            for i in range(n):
                ex(i)
                if i + 2 < n:
                    qk(i + 2)
                pv(i)
            return gs, t_g_last[0]

        def emit_attn_sb(l, h, ready, phase_waits, pset=0, gen=None):
            gs = 0
            qT, kT, szT, vsb = psets[pset]
            base = ready + consts_ready + phase_waits
            xb = banks[3]
            steps = []
            for j in range(NQ):
                for b_ in range(4 * j + 3, -1, -1):
                    steps.append((j, b_))
            n = len(steps)
            zb = [None] * n
            t_qk = [None] * n
            t_exp = [None] * n
            t_ln = [None] * n
            t_nt = [None] * n
            t_er = [None] * n
            t_ct = [None] * n
            t_mul = [None] * n
            oinfo = {}
            for j in range(NQ):
                oinfo[j] = 0
            t_g_last = [None]
            x_free = [st["bank_free"][3]]

            eps = [banks[2][:, :], tb_bf[0].bitcast(F32)[:, :], tb_bf[1].bitcast(F32)[:, :]]
            e_free = [st["bank_free"][2], st["tb_free"][0], st["tb_free"][1]]
            NE = 3

            def crange(j, b_):
                i_ = b_ - 4 * j
                return (128 * i_ if i_ > 0 else 0)

            def qk(i):
                j, b_ = steps[i]
                sbk = i % 2
                zb[i] = sbk
                band = b_ >= 4 * j
                c0 = crange(j, b_)
                tk = PE(lambda e: e.matmul(banks[sbk][:, c0:512], lhsT=kT[:, b_ * 128:(b_ + 1) * 128],
                                           rhs=qT[:, j * 512 + c0:(j + 1) * 512], start=True, stop=(not band)),
                        base + [st["bank_free"][sbk]], m=(not band))
                if band:
                    tk = PE(lambda e: e.matmul(banks[sbk][:, c0:c0 + 128], lhsT=ident_bf,
                                               rhs=masks_bf(b_ - 4 * j)[:, c0:c0 + 128],
                                               start=False, stop=True, skip_group_check=True), base, m=True)
                t_qk[i] = tk

            def a_exp(i):
                j, b_ = steps[i]
                sl = i % NE
                sbk = zb[i]
                c0 = crange(j, b_)
                t_exp[i] = ACT(lambda e: e.activation(out=eps[sl][:, c0:512], in_=banks[sbk][:, c0:512], func=AF.Exp),
                               [t_qk[i], e_free[sl]] + base)
                st["bank_free"][sbk] = t_exp[i]

            def a_ln(i):
                j, b_ = steps[i]
                sl = i % NSL
                c0 = crange(j, b_)
                t_ln[i] = ACT(lambda e: e.activation(out=sp_sl[sl][:, c0:512], in_=eps[i % NE][:, c0:512], func=AF.Ln,
                                                     bias=1.0, scale=1.0), [t_exp[i], spslot_free[sl]])

            def p_nt(i):
                j, b_ = steps[i]
                sl = i % NSL
                c0 = crange(j, b_)
                first = (b_ == 4 * j + 3)
                w = [t_ln[i]]
                if i > 0:
                    w.append(t_er[i - 1])
                if first:
                    w.append(x_free[0])
                t_nt[i] = PE(lambda e: e.matmul(xb[:, c0:512], lhsT=ntri_bf, rhs=sp_sl[sl][:, c0:512], start=first,
                                                stop=True, skip_group_check=True), w + base, m=True)

            def a_er(i):
                j, b_ = steps[i]
                sl = i % NSL
                c0 = crange(j, b_)
                t_er[i] = ACT(lambda e: e.activation(out=er_sl[sl][:, c0:512], in_=xb[:, c0:512], func=AF.Exp),
                              [t_nt[i], erslot_free[sl]])

            def p_ct(i):
                j, b_ = steps[i]
                sl = i % NSL
                c0 = crange(j, b_)
                if b_ == 0:
                    spslot_free[sl] = t_nt[i]
                    return
                t_ct[i] = PE(lambda e: e.matmul(xb[:, c0:512], lhsT=ctri_bf, rhs=sp_sl[sl][:, c0:512], start=False,
                                                stop=True, skip_group_check=True), [t_er[i]], m=True)
                spslot_free[sl] = t_ct[i]

            def d_mul(i):
                j, b_ = steps[i]
                sl = i % NSL
                c0 = crange(j, b_)
                t_mul[i] = DVE(lambda e: e.tensor_tensor(out=a_sl[sl][:, c0:512], in0=eps[i % NE][:, c0:512],
                                                         in1=er_sl[sl][:, c0:512], op=ALU.mult),
                               [t_er[i], aslot_free[sl]])
                e_free[i % NE] = t_mul[i]
                erslot_free[sl] = t_mul[i]

            def p_pv(i):
                j, b_ = steps[i]
                sl = i % NSL
                c0 = crange(j, b_)
                oi = oinfo[j]
                ob = banks[4 + oi]
                first = (b_ == 4 * j + 3)
                lastb = (b_ == 0)
                tk = PE(lambda e: e.matmul(ob[:, c0:512], lhsT=vsb[:, b_ * 128:(b_ + 1) * 128], rhs=a_sl[sl][:, c0:512],
                                           start=first, stop=lastb, skip_group_check=True),
                        [t_mul[i], o_free[oi] if first else None] + base, m=True)
                aslot_free[sl] = tk
                if lastb:
                    t_g = DVE(lambda e: e.tensor_tensor(out=gT[gs][:, j * 512:(j + 1) * 512], in0=ob[:, :],
                                                        in1=szT[:, j * 512:(j + 1) * 512], op=ALU.mult),
                              [tk, g_free[gs]] + base)
                    o_free[oi] = t_g
                    t_g_last[0] = t_g

            qk(0)
            if n > 1:
                qk(1)
            a_exp(0)
            if n > 1:
                a_exp(1)
            a_ln(0)
            p_nt(0)
            for i in range(n):
                if i + 2 < n:
                    qk(i + 2)
                a_er(i)
                if i + 1 < n:
                    a_ln(i + 1)
                if i + 2 < n:
                    a_exp(i + 2)
                p_ct(i)
                if i + 1 < n:
                    p_nt(i + 1)
                d_mul(i)
                if i >= 1:
                    p_pv(i - 1)
                if gen is not None:
                    next(gen, None)
                    if i % 8 == 4:
                        next(gen, None)
            p_pv(n - 1)
            if gen is not None:
                for _ in gen:
                    pass
            st["bank_free"][3] = t_er[n - 1]
            st["bank_free"][2] = t_mul[n - 1]
            st["tb_free"][0] = t_mul[n - 1]
            st["tb_free"][1] = t_mul[n - 1]
            return gs, t_g_last[0]

        cstate = {"xb_free": [None, None], "xnb_free": [None, None], "gb_free": [None, None],
                  "r_free": [None, None], "xnew_free": [[], []], "y_free": [None, None],
                  "wost_free": [None, None]}
        out_tickets = []

        def emit_phase_c(l, phase_waits):
            lastl = (l == DEPTH - 1)
            src = x_d if l == 0 else xres_d
            dst = out_d if lastl else xres_d
            wcast = []
            for h in range(H):
                i = h % 2
                t_ld = DMA(lambda e, h=h, i=i: e.dma_start(out=wost[i], in_=wo_d[l, :, h * D:(h + 1) * D]),
                           s_dwo[i], [cstate["wost_free"][i]] + phase_waits)
                t_c = POOL(lambda e, h=h, i=i: e.tensor_copy(out=wobf[:, h * D:(h + 1) * D], in_=wost[i]),
                           [t_ld] + phase_waits)
                cstate["wost_free"][i] = t_c
                wcast.append(t_c)
            t_g = DMA(lambda e: e.dma_start(out=gB, in_=lng_d[l]), s_lng, phase_waits)
            t_b = DMA(lambda e: e.dma_start(out=bB, in_=lnb_d[l]), s_lnb, phase_waits)
            nch = max(1, D // 512)
            csz = D // nch
            new_xT = [None] * NT
            loads = {}
            SW = nch * 6 + 8

            def issue_loads(tb):
                i = tb % 2
                t_x = DMA(lambda e: e.dma_start(out=xblk[i], in_=src[tb * 128:(tb + 1) * 128, :]),
                          s_dx[i], [cstate["xb_free"][i]] + phase_waits)
                t_gl = DMA(lambda e: e.dma_start(
                    out=gblk[i].rearrange("p (h c) -> p h c", c=128),
                    in_=gscr_d[:, :, tb * 128:(tb + 1) * 128].rearrange("h p c -> p h c")),
                    s_dg[i], [cstate["gb_free"][i]] + phase_waits)
                loads[tb] = (t_x, t_gl)

            def sviews(i):
                stv = stats[:, i * SW: i * SW + nch * 6]
                mv = stats[:, i * SW + nch * 6: i * SW + nch * 6 + 2]
                lnv = stats[:, i * SW + nch * 6 + 2: i * SW + nch * 6 + 3]
                rstd = stats[:, i * SW + nch * 6 + 3: i * SW + nch * 6 + 4]
                return stv, mv, lnv, rstd

            tkA = {}
            tB = {}
            tC = {}
            stat_free = cstate.setdefault("stat_free", [None, None])

            def stA(tb):
                i = tb % 2
                t_x, t_gl = loads[tb]
                gv = gblk[i].rearrange("p (h c) -> p h c", c=128)
                ybanks = [banks[2 * i + k] for k in range(nch)]
                tk = None
                for k in range(nch):
                    for h in range(H):
                        tk = PE(lambda e, k=k, h=h: e.matmul(
                            ybanks[k][:, :csz], lhsT=gv[:, h, :], rhs=wobf[:, h * D + k * csz: h * D + (k + 1) * csz],
                            start=(h == 0), stop=(h == H - 1)),
                            [t_gl, cstate["y_free"][i]] + wcast + phase_waits, m=(h == H - 1 and k == nch - 1))
                cstate["gb_free"][i] = tk
                tkA[tb] = tk

            def stB(tb):
                i = tb % 2
                t_x, t_gl = loads[tb]
                ybanks = [banks[2 * i + k] for k in range(nch)]
                stv, mv, lnv, rstd = sviews(i)
                t_r = None
                for k in range(nch):
                    t_r = DVE(lambda e, k=k: e.scalar_tensor_tensor(
                        out=rbuf[i][:, k * csz:(k + 1) * csz], in0=xblk[i][:, k * csz:(k + 1) * csz],
                        scalar=float(cfg.alpha), in1=ybanks[k][:, :csz], op0=ALU.mult, op1=ALU.add),
                        [tkA[tb], t_x, cstate["r_free"][i]] + phase_waits)
                cstate["xb_free"][i] = t_r
                cstate["y_free"][i] = t_r
                t_s = None
                for k in range(nch):
                    t_s = DVE(lambda e, k=k: e.bn_stats(out=stv[:, k * 6:(k + 1) * 6],
                                                        in_=rbuf[i][:, k * csz:(k + 1) * csz]), [t_r])
                t_a = DVE(lambda e: e.bn_aggr(out=mv, in_=stv), [t_s])
                t_l1 = ACT(lambda e: e.activation(out=lnv, in_=mv[:, 1:2], func=AF.Ln, bias=float(LN_EPS), scale=1.0),
                           [t_a, stat_free[i]])
                t_l2 = ACT(lambda e: e.activation(out=rstd, in_=lnv, func=AF.Exp, scale=-0.5), [t_l1])
                tB[tb] = (t_a, t_l2)

            def stC(tb):
                i = tb % 2
                stv, mv, lnv, rstd = sviews(i)
                t_a, t_l2 = tB[tb]
                t_n = DVE(lambda e: e.tensor_scalar(
                    out=rbuf[i], in0=rbuf[i], scalar1=mv[:, 0:1], scalar2=rstd, op0=ALU.subtract, op1=ALU.mult),
                    [t_l2, t_a])
                stat_free[i] = t_n
                t_m = POOL(lambda e: e.tensor_tensor(out=rbuf[i], in0=rbuf[i], in1=gB, op=ALU.mult),
                           [t_n, t_g] + phase_waits)
                t_o = POOL(lambda e: e.tensor_tensor(out=xnew[i], in0=rbuf[i], in1=bB, op=ALU.add),
                           [t_m, t_b] + cstate["xnew_free"][i] + phase_waits)
                cstate["r_free"][i] = t_o
                t_st = DMA(lambda e: e.dma_start(out=dst[tb * 128:(tb + 1) * 128, :], in_=xnew[i]), s_out[i], [t_o])
                out_tickets.append(t_st)
                if not lastl:
                    t_cast = ACT(lambda e: e.activation(out=xnb[i], in_=xnew[i], func=AF.Copy),
                                 [t_o, cstate["xnb_free"][i]])
                    cstate["xnew_free"][i] = [t_st, t_cast]
                    tC[tb] = t_cast
                else:
                    cstate["xnew_free"][i] = [t_st]

            def stD(tb):
                if lastl:
                    return
                i = tb % 2
                tkt = None
                for kc in range(KD):
                    tkt = PE(lambda e, kc=kc: e.transpose(out=tb_bf[i][:, kc * 128:(kc + 1) * 128],
                                                          in_=xnb[i][:, kc * 128:(kc + 1) * 128], identity=ident_bf),
                             [tC[tb], st["tb_free"][i]] + consts_ready + phase_waits, m=(kc == KD - 1))
                cstate["xnb_free"][i] = tkt
                dstx = xT[:, :, tb * 128:(tb + 1) * 128]
                srcx = tb_bf[i][:, :KD * 128].rearrange("p (k c) -> p k c", c=128)
                t_cp = DVE(lambda e: e.tensor_copy(out=dstx, in_=srcx), [tkt])
                st["tb_free"][i] = t_cp
                st["xT_ready"][tb] = t_cp
                new_xT[tb] = t_cp

            issue_loads(0)
            if NT > 1:
                issue_loads(1)
            stA(0)
            for t in range(NT + 2):
                if t + 1 < NT:
                    stA(t + 1)
                if t < NT:
                    stB(t)
                    if t + 2 < NT:
                        issue_loads(t + 2)
                if 0 <= t - 1 < NT:
                    stC(t - 1)
                if 0 <= t - 2 < NT:
                    stD(t - 2)
            return new_xT

        phase_waits = []
        phase_waits = [t for t in all_xT[-2:]] + consts_ready
        emit_wload(0, 0)
        gstore = [None]

        def nxt_of(l, h):
            return (l, h + 1) if h + 1 < H else ((l + 1, 0) if l + 1 < DEPTH else None)

        def run_all(g):
            for _ in g:
                pass

        for l in range(DEPTH):
            is_fox = (cfg.mixers[l] == 'fox')
            pw = list(phase_waits)
            fox_ready = []
            set_done = [[], []]
            if is_fox:
                fox_ready = emit_fox_pre(l, pw)
                pw = pw + fox_ready
                for h in range(H):
                    emit_wcast(l, h)
                    nxt = nxt_of(l, h)
                    if nxt is not None:
                        emit_wload(nxt[0], nxt[1], extra_waits=wl["cast"][(l, h)])
                    outd = {}
                    run_all(proj_gen(l, h, 0, pw + set_done[0], False, outd))
                    gs, t_g = emit_attn_fox(l, h, outd["ready"], fox_ready, pw, pset=0)
                    set_done[0] = [t_g, (s_pe, s_pe.v)]
                    t_store = DMA(lambda e, h=h: e.dma_start(out=gscr_d[h], in_=gT[0]), s_gst[0], [t_g])
                    g_free[0] = t_store
                    gstore[0] = t_store
            else:
                emit_wcast(l, 0)
                nxt = nxt_of(l, 0)
                if nxt is not None:
                    emit_wload(nxt[0], nxt[1], extra_waits=wl["cast"][(l, 0)])
                outd = {}
                run_all(proj_gen(l, 0, 0, pw, False, outd))
                ready = outd["ready"]
                for h in range(H):
                    gen = None
                    outn = {}
                    if h + 1 < H:
                        emit_wcast(l, h + 1, pool_only=True)
                        nxt = nxt_of(l, h + 1)
                        if nxt is not None:
                            emit_wload(nxt[0], nxt[1], extra_waits=wl["cast"][(l, h + 1)])
                        gen = proj_gen(l, h + 1, (h + 1) % 2, pw + set_done[(h + 1) % 2], True, outn)
                    gs, t_g = emit_attn_sb(l, h, ready, pw, pset=h % 2, gen=gen)
                    set_done[h % 2] = [t_g, (s_pe, s_pe.v)]
                    t_store = DMA(lambda e, h=h: e.dma_start(out=gscr_d[h], in_=gT[0]), s_gst[0], [t_g])
                    g_free[0] = t_store
                    gstore[0] = t_store
                    ready = outn.get("ready")
            st["attn_done"] = set_done[0] + set_done[1]
            pwc = [t for t in gstore if t is not None] + st["attn_done"] + [(s_act, s_act.v), (s_dve, s_dve.v)]
            new_xT = emit_phase_c(l, pwc)
            if l + 1 < DEPTH:
                all_xT[:] = new_xT
                phase_waits = [(s_pe, s_pe.v), (s_act, s_act.v), (s_dve, s_dve.v), (s_pool, s_pool.v)] + \
                    out_tickets[-2:] + consts_ready
                st["attn_done"] = []
        R.op("sync", None, [(s_out[0], s_out[0].v), (s_out[1], s_out[1].v)])
        if DBG:
            R.op("gpsimd", None, [(s_dbg, s_dbg.v)])

        with nc.Block() as block:
            @block.sync
            def _(e):
                R.replay("sync", e)

            @block.tensor
            def _(e):
                R.replay("tensor", e)

            @block.vector
            def _(e):
                R.replay("vector", e)

            @block.scalar
            def _(e):
                R.replay("scalar", e)

            @block.gpsimd
            def _(e):
                R.replay("gpsimd", e)
    return nc


def prepare_inputs(cfg, x, fox_w_in, fox_b_f, fox_w_out, sb_w_in, sb_w_out, ln_g, ln_b):
    w_in_layers, w_out_layers = [], []
    for l in range(cfg.DEPTH):
        slot = l // 2
        if l % 2 == 0:
            w_in_layers.append(np.asarray(fox_w_in[slot]))
            w_out_layers.append(np.asarray(fox_w_out[slot]))
        else:
            w_in_layers.append(np.asarray(sb_w_in[slot]))
            w_out_layers.append(np.asarray(sb_w_out[slot]))
    wq, wo = layout_weights(cfg, w_in_layers, w_out_layers)
    wf = layout_wf(cfg, np.asarray(fox_w_in))
    bf = np.ascontiguousarray(np.asarray(fox_b_f, np.float32).reshape(cfg.NFOX, cfg.H, 1))
    lng = np.ascontiguousarray(np.broadcast_to(np.asarray(ln_g, np.float32)[:, None, :], (cfg.DEPTH, 128, cfg.D)))
    lnb = np.ascontiguousarray(np.broadcast_to(np.asarray(ln_b, np.float32)[:, None, :], (cfg.DEPTH, 128, cfg.D)))
    cb, cf = make_consts(cfg)
    shared = {"wq": wq, "wo": wo, "wf": wf, "bf": bf, "lng": lng, "lnb": lnb, "cb": cb, "cf": cf}
    in_maps = []
    for b in range(x.shape[0]):
        m = dict(shared)
        m["x"] = np.ascontiguousarray(np.asarray(x[b], np.float32))
        in_maps.append(m)
    return in_maps


_CACHE = {}


def kernel(x, fox_w_in, fox_b_f, fox_w_out, sb_w_in, sb_w_out, ln_g, ln_b):
    cfg = Cfg()
    x = np.asarray(x)
    in_maps = prepare_inputs(cfg, x, fox_w_in, fox_b_f, fox_w_out, sb_w_in, sb_w_out, ln_g, ln_b)
    if "nc" not in _CACHE:
        _CACHE["nc"] = build_program(cfg)
    nc = _CACHE["nc"]
    res = run_bass_kernel_spmd(nc, in_maps, core_ids=list(range(8)))
    out = np.stack([np.asarray(r["out"]) for r in res.results], axis=0)
    return out.astype(np.float32)
```

```python
import numpy as np
import concourse.bass as bass
import concourse.mybir as mybir
from concourse.bass_utils import run_bass_kernel_spmd
from contextlib import ExitStack

F32 = mybir.dt.float32
BF16 = mybir.dt.bfloat16
AF = mybir.ActivationFunctionType
ALU = mybir.AluOpType

LN_EPS = 1e-5
MASKV = -30000.0


class Cfg:
    def __init__(self, S=4096, D=1024, H=16, DEPTH=4):
        self.S, self.D, self.H, self.DEPTH = S, D, H, DEPTH
        self.dh = 128
        self.KD = D // 128
        self.NT = S // 128
        self.NQ = S // 512
        self.DI = H * 128
        self.alpha = (2 * DEPTH) ** 0.25
        self.scale = 128 ** -0.5
        self.NFOX = (DEPTH + 1) // 2
        self.NSB = DEPTH // 2
        self.mixers = ['fox' if l % 2 == 0 else 'sb' for l in range(DEPTH)]
        self.c_ident = 0
        self.c_ntri = 128
        self.c_ctri = 256
        self.c_ones = 384
        self.c_maskf = 512
        self.c_masks = 512 + 2048
        self.c_sel = 512 + 4096
        self.NCB = 512 + 4096 + H * 128


def make_consts(cfg):
    c = np.zeros((128, cfg.NCB), np.float32)
    j = np.arange(128)[:, None]
    s = np.arange(128)[None, :]
    c[:, cfg.c_ident:cfg.c_ident + 128] = (j == s)
    c[:, cfg.c_ntri:cfg.c_ntri + 128] = -(j >= s).astype(np.float32)
    c[:, cfg.c_ctri:cfg.c_ctri + 128] = -(j < s).astype(np.float32)
    c[:, cfg.c_ones:cfg.c_ones + 128] = 1.0
    tq = np.arange(512)[None, :]
    for i in range(4):
        key = 128 * i + np.arange(128)[:, None]
        c[:, cfg.c_maskf + i * 512: cfg.c_maskf + (i + 1) * 512] = np.where(key <= tq, 0.0, MASKV)
        c[:, cfg.c_masks + i * 512: cfg.c_masks + (i + 1) * 512] = np.where(key < tq, 0.0, MASKV)
    for h in range(cfg.H):
        c[h, cfg.c_sel + h * 128: cfg.c_sel + (h + 1) * 128] = 1.0
    cf = np.zeros((128, 64), np.float32)
    cf[:, 0] = 1.0
    cf[:, 1] = LN_EPS
    cf[:, 2] = 0.0
    cf[:32, 8:40] = np.eye(32, dtype=np.float32)
    return c, cf


def layout_weights(cfg, w_in_layers, w_out_layers):
    KD, H, DI, D = cfg.KD, cfg.H, cfg.DI, cfg.D
    wq = np.empty((cfg.DEPTH, H, 128, 4 * KD * 128), np.float32)
    wo = np.empty((cfg.DEPTH, 128, H * D), np.float32)
    for l in range(cfg.DEPTH):
        w = w_in_layers[l][:, :4 * DI].reshape(KD, 128, 4, H, 128)
        wq[l] = np.transpose(w, (3, 1, 2, 0, 4)).reshape(H, 128, 4 * KD * 128)
        o = w_out_layers[l].reshape(H, 128, D)
        wo[l] = np.transpose(o, (1, 0, 2)).reshape(128, H * D)
    return wq, wo


def layout_wf(cfg, fox_w_in):
    KD, H, DI = cfg.KD, cfg.H, cfg.DI
    n = fox_w_in.shape[0]
    wf = np.empty((n, 128, KD * H), np.float32)
    for s in range(n):
        w = fox_w_in[s][:, 4 * DI:4 * DI + H].reshape(KD, 128, H)
        wf[s] = np.transpose(w, (1, 0, 2)).reshape(128, KD * H)
    return wf


ENGS = ("sync", "tensor", "vector", "scalar", "gpsimd")


class Sem:
    def __init__(self, h, name):
        self.h = h
        self.v = 0
        self.name = name


class Rec:
    def __init__(self, nc, stack):
        self.nc = nc
        self.stack = stack
        self.q = {e: [] for e in ENGS}
        self.waited = {}
        self.nsem = 0

    def sem(self, name):
        h = self.stack.enter_context(self.nc.semaphore(name))
        self.nsem += 1
        return Sem(h, name)

    def op(self, eng, fn, waits=(), inc=None, incv=None):
        mx = {}
        for t in waits:
            if t is None:
                continue
            sem, val = t
            if val <= 0:
                continue
            if id(sem) not in mx or mx[id(sem)][1] < val:
                mx[id(sem)] = (sem, val)
        ws = []
        for sem, val in mx.values():
            key = (eng, id(sem))
            if self.waited.get(key, 0) >= val:
                continue
            self.waited[key] = val
            ws.append((sem, val))
        tick = None
        if inc is not None:
            if incv is None:
                incv = 1
            inc.v += incv
            tick = (inc, inc.v)
        self.q[eng].append((ws, fn, inc, incv))
        return tick

    def replay(self, eng, e):
        for ws, fn, inc, incv in self.q[eng]:
            for sem, val in ws:
                e.wait_ge(sem.h, val)
            if fn is None:
                continue
            ins = fn(e)
            if inc is not None:
                ins.then_inc(inc.h, incv)


def build_program(cfg):
    S, D, H, DEPTH = cfg.S, cfg.D, cfg.H, cfg.DEPTH
    KD, NT, NQ, DI = cfg.KD, cfg.NT, cfg.NQ, cfg.DI
    nc = bass.Bass("TRN2", target_bir_lowering=False)

    x_d = nc.dram_tensor("x", [S, D], F32, kind="ExternalInput").ap()
    wq_d = nc.dram_tensor("wq", [DEPTH, H, 128, 4 * KD * 128], F32, kind="ExternalInput").ap()
    wo_d = nc.dram_tensor("wo", [DEPTH, 128, H * D], F32, kind="ExternalInput").ap()
    wf_d = nc.dram_tensor("wf", [cfg.NFOX, 128, KD * H], F32, kind="ExternalInput").ap()
    bf_d = nc.dram_tensor("bf", [cfg.NFOX, H, 1], F32, kind="ExternalInput").ap()
    lng_d = nc.dram_tensor("lng", [DEPTH, 128, D], F32, kind="ExternalInput").ap()
    lnb_d = nc.dram_tensor("lnb", [DEPTH, 128, D], F32, kind="ExternalInput").ap()
    cb_d = nc.dram_tensor("cb", [128, cfg.NCB], F32, kind="ExternalInput").ap()
    cf_d = nc.dram_tensor("cf", [128, 64], F32, kind="ExternalInput").ap()
    out_d = nc.dram_tensor("out", [S, D], F32, kind="ExternalOutput").ap()
    xres_d = nc.dram_tensor("xres", [S, D], F32, kind="Internal").ap()
    gscr_d = nc.dram_tensor("gscr", [H, 128, S], BF16,
                            kind="ExternalOutput" if getattr(cfg, "debug", False) else "Internal").ap()

    DBG = getattr(cfg, "debug", False)
    if DBG:
        dbgf_d = nc.dram_tensor("dbgf", [8, 2, 128, 512], F32, kind="ExternalOutput").ap()
        dbgb_d = nc.dram_tensor("dbgb", [8, 2, 128, 512], BF16, kind="ExternalOutput").ap()
    with ExitStack() as stack:
        def sb(name, shape, dt):
            return stack.enter_context(nc.sbuf_tensor(name, shape, dt))

        def ps(name, shape, dt):
            return stack.enter_context(nc.psum_tensor(name, shape, dt))

        R = Rec(nc, stack)

        xT = sb("xT", [128, KD, S], BF16)
        cbf = sb("cbf", [128, cfg.NCB], BF16)
        cf = sb("cf32", [128, 64], F32)
        wstage = sb("wstage", [128, 4 * KD * 128], F32)
        whbf = [sb(f"whbf{i}", [128, 4, KD, 128], BF16) for i in range(2)]
        ARENA = 91136
        arena = sb("arena", [128, ARENA // 4], F32)

        class Carver:
            def __init__(self):
                self.off = 0

            def take(self, nbytes, dt, shape):
                assert self.off % 4 == 0
                nb = (nbytes + 31) // 32 * 32
                a = arena[:, self.off // 4:(self.off + nb) // 4]
                self.off += nb
                assert self.off <= ARENA, (self.off, ARENA)
                if dt == BF16:
                    a = a.bitcast(BF16)[:, :nbytes // 2]
                else:
                    a = a[:, :nbytes // 4]
                return a

        ca = Carver()
        psets = []
        set0 = [ca.take(S * 2, BF16, None) for _ in range(4)]
        psets.append(set0)
        gT = [ca.take(S * 2, BF16, None)]
        NSL = getattr(cfg, 'NSL', 3)
        NES = 2
        e_sl = [ca.take(2048, F32, None) for _ in range(NES)]
        er_sl = [ca.take(2048, F32, None) for _ in range(NSL)]
        sp_sl = [ca.take(1024, BF16, None) for _ in range(NSL)]
        a_sl = [ca.take(1024, BF16, None) for _ in range(NSL)]
        set1_off = ca.off
        set1 = [ca.take(S * 2, BF16, None) for _ in range(4)]
        psets.append(set1)
        attn_end_sb = ca.off
        ca.off = set1_off
        rinv = ca.take(2048, F32, None)
        ttmp = ca.take(2048, F32, None)
        Gall = ca.take(S * 2, BF16, None)
        negc_tm = ca.take(NT * H * 4, F32, None)
        negb = ca.take(32, F32, None)
        bfcol = ca.take(32, F32, None)
        wfst = ca.take(KD * H * 4, F32, None)
        wfbf = ca.take(KD * H * 2, BF16, None)
        assert ca.off <= attn_end_sb
        ca.off = attn_end_sb
        attn_end = ca.off
        cfx = Carver()
        e_fm = cfx.take(S * 4, F32, None)
        negc_fm = cfx.take(S * 4, F32, None)
        assert cfx.off <= 4 * S * 2
        cst = Carver()
        cstage = cst.take(cfg.NCB * 4, F32, None)
        assert cst.off <= ARENA
        cc = Carver()
        wobf = cc.take(H * D * 2, BF16, None)
        wost = [cc.take(D * 4, F32, None) for _ in range(2)]
        xblk = [cc.take(D * 4, F32, None) for _ in range(2)]
        rbuf = [cc.take(D * 4, F32, None) for _ in range(2)]
        xnew = [cc.take(D * 4, F32, None) for _ in range(2)]
        xnb = [cc.take(D * 2, BF16, None) for _ in range(2)]
        gblk = [cc.take(H * 128 * 2, BF16, None) for _ in range(2)]
        gB = cc.take(D * 4, F32, None)
        bB = cc.take(D * 4, F32, None)
        stats = cc.take(2 * (D // 512 if D >= 512 else 1) * 6 * 4 + 64, F32, None)

        banks = [ps(f"pb{i}", [128, 512], F32) for i in range(6)]
        tb_bf = [ps(f"ptb{i}", [128, 1024], BF16) for i in range(2)]

        one_col = cf[:, 0:1]
        eps_col = cf[:, 1:2]
        ident32 = cf[0:32, 8:40]
        ident_bf = cbf[:, cfg.c_ident:cfg.c_ident + 128]
        ntri_bf = cbf[:, cfg.c_ntri:cfg.c_ntri + 128]
        ctri_bf = cbf[:, cfg.c_ctri:cfg.c_ctri + 128]
        ones_bf = cbf[:, cfg.c_ones:cfg.c_ones + 128]

        def maskf_bf(i):
            return cbf[:, cfg.c_maskf + i * 512: cfg.c_maskf + (i + 1) * 512]

        def masks_bf(i):
            return cbf[:, cfg.c_masks + i * 512: cfg.c_masks + (i + 1) * 512]

        def sel_bf(h):
            return cbf[:, cfg.c_sel + h * 128: cfg.c_sel + (h + 1) * 128]

        s_pe = R.sem("s_pe")
        s_act = R.sem("s_act")
        s_dve = R.sem("s_dve")
        s_pool = R.sem("s_pool")
        s_dw = R.sem("s_dw")
        s_cst = R.sem("s_cst")
        s_cf = R.sem("s_cf")
        s_wf = R.sem("s_wf")
        s_bf = R.sem("s_bf")
        s_lng = R.sem("s_lng")
        s_lnb = R.sem("s_lnb")
        s_dx = [R.sem(f"s_dx{i}") for i in range(2)]
        s_dg = [R.sem(f"s_dg{i}") for i in range(2)]
        s_dwo = [R.sem(f"s_dwo{i}") for i in range(2)]
        s_gst = [R.sem(f"s_gst{i}") for i in range(2)]
        s_out = [R.sem(f"s_out{i}") for i in range(2)]
        s_dbg = R.sem("s_dbg")

        def PE(fn, waits=(), m=False):
            return R.op("tensor", fn, waits, s_pe if m else None)

        def ACT(fn, waits=(), m=True):
            return R.op("scalar", fn, waits, s_act if m else None)

        def DVE(fn, waits=(), m=True):
            return R.op("vector", fn, waits, s_dve if m else None)

        def POOL(fn, waits=(), m=True):
            return R.op("gpsimd", fn, waits, s_pool if m else None)

        def DMA(fn, sem, waits=()):
            return R.op("sync", fn, waits, sem, 16)

        st = {
            "bank_free": [None] * 6,
            "tb_free": [None, None],
            "xT_ready": [None] * NT,
            "wstage_free": None,
            "whbf_free": [None, None],
            "last": [],
        }

        t_cst = DMA(lambda e: e.dma_start(out=cstage, in_=cb_d[:, :]), s_cst)
        t_cf = DMA(lambda e: e.dma_start(out=cf[:], in_=cf_d[:, :]), s_cf)
        half = cfg.NCB // 2
        t_c1 = DVE(lambda e: e.tensor_copy(out=cbf[:, :half], in_=cstage[:, :half]), [t_cst])
        t_c2 = POOL(lambda e: e.tensor_copy(out=cbf[:, half:], in_=cstage[:, half:]), [t_cst])
        consts_ready = [t_c1, t_c2, t_cf]

        def emit_xT_block(tb, src_f32, src_ticket, castbuf, cast_free_ticket, extra_waits=()):
            i = tb % 2
            t_cast = ACT(lambda e: e.activation(out=castbuf, in_=src_f32, func=AF.Copy),
                         [src_ticket, cast_free_ticket] + list(extra_waits))
            tkt = None
            for kc in range(KD):
                last = kc == KD - 1
                tkt = PE(lambda e, kc=kc: e.transpose(out=tb_bf[i][:, kc * 128:(kc + 1) * 128],
                                                      in_=castbuf[:, kc * 128:(kc + 1) * 128],
                                                      identity=ident_bf),
                         [t_cast, st["tb_free"][i]] + consts_ready, m=last)
            dst = xT[:, :, tb * 128:(tb + 1) * 128]
            src = tb_bf[i][:, :KD * 128].rearrange("p (k c) -> p k c", c=128)
            t_cp = DVE(lambda e: e.tensor_copy(out=dst, in_=src), [tkt])
            st["tb_free"][i] = t_cp
            st["xT_ready"][tb] = t_cp
            return tkt, t_cp

        xb_free = [None, None]
        xnb_free = [None, None]
        for tb in range(NT):
            i = tb % 2
            t_ld = DMA(lambda e, tb=tb, i=i: e.dma_start(out=xblk[i], in_=x_d[tb * 128:(tb + 1) * 128, :]),
                       s_dx[i], [xb_free[i], t_c1, t_c2])
            t_pe, t_cp = emit_xT_block(tb, xblk[i], t_ld, xnb[i], xnb_free[i])
            xb_free[i] = t_pe
            xnb_free[i] = t_pe
        st["last"] = [st["xT_ready"][NT - 1], st["xT_ready"][NT - 2]]
        all_xT = list(st["xT_ready"])

        wl = {"dma": {}, "cast": {}}

        def emit_wload(l, h, extra_waits=()):
            t = DMA(lambda e: e.dma_start(out=wstage[:], in_=wq_d[l, h]), s_dw,
                    [st["wstage_free"]] + list(extra_waits))
            wl["dma"][(l, h)] = t

        def emit_wcast(l, h, pool_only=False):
            slot = (l * H + h) % 2
            src = wstage[:].rearrange("p (t k c) -> p t k c", t=4, k=KD)
            w = [wl["dma"][(l, h)], st["whbf_free"][slot]]
            if pool_only:
                t1 = POOL(lambda e: e.tensor_copy(out=whbf[slot][:, 0:2], in_=src[:, 0:2]), w)
                t2 = POOL(lambda e: e.tensor_copy(out=whbf[slot][:, 2:4], in_=src[:, 2:4]), w)
            else:
                t1 = POOL(lambda e: e.tensor_copy(out=whbf[slot][:, 0:2], in_=src[:, 0:2]), w)
                t2 = DVE(lambda e: e.tensor_copy(out=whbf[slot][:, 2:4], in_=src[:, 2:4]), w)
            st["wstage_free"] = None
            wl["cast"][(l, h)] = [t1, t2]
            return [t1, t2]

        rot = {"i": 0}

        def next_sbank():
            b = rot["i"] % 3
            rot["i"] += 1
            return b

        def emit_fox_pre(l, phase_waits):
            slot = l // 2
            t_wf = DMA(lambda e: e.dma_start(out=wfst, in_=wf_d[slot]), s_wf, phase_waits)
            t_bf = DMA(lambda e: e.dma_start(out=bfcol[0:H, 0:1], in_=bf_d[slot]), s_bf, phase_waits)
            t_wfc = DVE(lambda e: e.tensor_copy(out=wfbf, in_=wfst), [t_wf] + phase_waits)
            t_nb = DVE(lambda e: e.tensor_scalar(out=negb[0:H, 0:1], in0=bfcol[0:H, 0:1], scalar1=-1.0,
                                                 scalar2=None, op0=ALU.mult), [t_bf])
            wfv = wfbf.rearrange("p (k h) -> p k h", h=H)
            t_e = None
            for t in range(NQ):
                b = next_sbank()
                tk = None
                for kc in range(KD):
                    tk = PE(lambda e, kc=kc, t=t, b=b: e.matmul(
                        banks[b][0:H, :], lhsT=wfv[:, kc, :], rhs=xT[:, kc, t * 512:(t + 1) * 512],
                        start=(kc == 0), stop=(kc == KD - 1)),
                        [t_wfc, st["bank_free"][b]] + all_xT + phase_waits, m=(kc == KD - 1))
                t_e = ACT(lambda e, t=t, b=b: e.activation(
                    out=e_fm[0:H, t * 512:(t + 1) * 512], in_=banks[b][0:H, :], func=AF.Exp,
                    bias=negb[0:H, 0:1], scale=-1.0), [tk, t_nb] + phase_waits)
                st["bank_free"][b] = t_e
            t_sp = ACT(lambda e: e.activation(out=e_fm[0:H, :], in_=e_fm[0:H, :], func=AF.Ln,
                                              bias=1.0, scale=1.0), [t_e])
            t_sc = DVE(lambda e: e.tensor_tensor_scan(out=negc_fm[0:H, :], data0=e_fm[0:H, :], data1=e_fm[0:H, :],
                                                      initial=0.0, op0=ALU.add, op1=ALU.max), [t_sp])
            t_z = POOL(lambda e: e.memset(Gall, 0.0), phase_waits)
            t_g = DVE(lambda e: e.tensor_scalar(out=Gall[0:H, :], in0=negc_fm[0:H, :], scalar1=-1.0,
                                                scalar2=None, op0=ALU.mult), [t_sc, t_z])
            tk = None
            for blk in range(NT):
                tk = PE(lambda e, blk=blk: e.transpose(out=banks[3][:, blk * H:(blk + 1) * H],
                                                       in_=negc_fm[0:H, blk * 128:(blk + 1) * 128],
                                                       identity=ident32[0:H, 0:H]),
                        [t_sc, st["bank_free"][3]] + consts_ready, m=(blk == NT - 1))
            t_nc = DVE(lambda e: e.tensor_copy(out=negc_tm, in_=banks[3][:, :NT * H]), [tk])
            st["bank_free"][3] = t_nc
            return [t_g, t_nc]

        def proj_gen(l, h, pset, phase_waits, ovl, outd):
            slot = (l * H + h) % 2
            W = whbf[slot]
            wc = wl["cast"][(l, h)]
            qT_, kT_, szT_, vsb_ = psets[pset]
            base_w = wc + all_xT + phase_waits
            res = {}

            def getbank():
                return 5 if ovl else next_sbank()

            for which in (0, 1, 3):
                for t in range(NQ):
                    b = getbank()
                    tk = None
                    for kc in range(KD):
                        tk = PE(lambda e, kc=kc, t=t, b=b, which=which: e.matmul(
                            banks[b][:, :], lhsT=W[:, which, kc, :], rhs=xT[:, kc, t * 512:(t + 1) * 512],
                            start=(kc == 0), stop=(kc == KD - 1)),
                            base_w + [st["bank_free"][b]], m=(kc == KD - 1))
                        if ovl and kc == KD // 2 - 1:
                            yield
                    if ovl:
                        yield
                    if which == 0:
                        ev = DVE(lambda e, t=t, b=b: e.tensor_scalar(
                            out=qT_[:, t * 512:(t + 1) * 512], in0=banks[b][:, :], scalar1=cfg.scale,
                            scalar2=None, op0=ALU.mult), [tk] + phase_waits)
                        res["q"] = ev
                    elif which == 1:
                        if ovl:
                            ev = DVE(lambda e, t=t, b=b: e.tensor_copy(
                                out=kT_[:, t * 512:(t + 1) * 512], in_=banks[b][:, :]), [tk] + phase_waits)
                            res["k_dve"] = ev
                        else:
                            ev = ACT(lambda e, t=t, b=b: e.activation(
                                out=kT_[:, t * 512:(t + 1) * 512], in_=banks[b][:, :], func=AF.Copy),
                                [tk] + phase_waits)
                            res["k"] = ev
                    else:
                        sl = t % NES
                        t1 = ACT(lambda e, b=b, sl=sl: e.activation(out=e_sl[sl], in_=banks[b][:, :], func=AF.Exp,
                                                                    scale=-1.0), [tk, eslot_free[sl]] + phase_waits)
                        if ovl:
                            yield
                        t2 = ACT(lambda e, sl=sl: e.activation(out=e_sl[sl], in_=e_sl[sl], func=AF.Ln,
                                                               bias=1.0, scale=1.0), [t1])
                        if ovl:
                            yield
                        t3 = ACT(lambda e, sl=sl: e.activation(out=e_sl[sl], in_=e_sl[sl], func=AF.Exp, scale=-1.0),
                                 [t2])
                        if ovl:
                            yield
                        ev = DVE(lambda e, t=t, b=b, sl=sl: e.tensor_tensor(
                            out=szT_[:, t * 512:(t + 1) * 512], in0=banks[b][:, :], in1=e_sl[sl], op=ALU.mult),
                            [t3] + phase_waits)
                        eslot_free[sl] = ev
                        res["z"] = ev
                    st["bank_free"][b] = ev
                    yield
            tk = None
            for g4 in range(NT // 4):
                b = getbank()
                for i4 in range(4):
                    blk = g4 * 4 + i4
                    for kc in range(KD):
                        last = (i4 == 3 and kc == KD - 1)
                        tk = PE(lambda e, kc=kc, blk=blk, b=b, i4=i4: e.matmul(
                            banks[b][:, i4 * 128:(i4 + 1) * 128], lhsT=xT[:, kc, blk * 128:(blk + 1) * 128],
                            rhs=W[:, 2, kc, :], start=(kc == 0), stop=(kc == KD - 1)),
                            base_w + [st["bank_free"][b]], m=last)
                    if ovl and i4 == 1:
                        yield
                if ovl:
                    yield
                if ovl or g4 % 2 == 0:
                    ev = DVE(lambda e, g4=g4, b=b: e.tensor_copy(
                        out=vsb_[:, g4 * 512:(g4 + 1) * 512], in_=banks[b][:, :]), [tk] + phase_waits)
                    res["v_dve"] = ev
                else:
                    ev = ACT(lambda e, g4=g4, b=b: e.activation(
                        out=vsb_[:, g4 * 512:(g4 + 1) * 512], in_=banks[b][:, :], func=AF.Copy),
                        [tk] + phase_waits)
                    res["v_act"] = ev
                st["bank_free"][b] = ev
                yield
            st["whbf_free"][slot] = tk
            outd["ready"] = [res.get(k) for k in ("q", "k", "k_dve", "z", "v_dve", "v_act")]

        gcount = {"n": 0}
        osel = {"n": 0}
        o_free = [None, None]
        g_free = [None, None]
        aslot_free = [None] * NSL
        eslot_free = [None] * NES
        erslot_free = [None] * NSL
        spslot_free = [None] * NSL

        def emit_attn_fox(l, h, ready, fox_ready, phase_waits, pset=0):
            gs = 0
            qT, kT, szT, vsb = psets[pset]
            base = ready + fox_ready + consts_ready + phase_waits
            rsb = banks[3]
            steps = []
            for j in range(NQ):
                for b_ in range(4 * j + 4):
                    steps.append((j, b_))
            n = len(steps)
            oinfo = {}
            for j in range(NQ):
                oinfo[j] = osel["n"] % 2
                osel["n"] += 1
            zb = [None] * n
            t_qk = [None] * n
            t_ex = [None] * n
            t_g_last = [None]

            def crange(j, b_):
                i_ = b_ - 4 * j
                return (128 * i_ if i_ > 0 else 0)

            def qk(i):
                j, b_ = steps[i]
                sbk = next_sbank()
                zb[i] = sbk
                band = b_ >= 4 * j
                c0 = crange(j, b_)
                PE(lambda e: e.matmul(banks[sbk][:, c0:512], lhsT=kT[:, b_ * 128:(b_ + 1) * 128],
                                      rhs=qT[:, j * 512 + c0:(j + 1) * 512], start=True, stop=False),
                   base + [st["bank_free"][sbk]])
                tk = PE(lambda e: e.matmul(banks[sbk][:, c0:512], lhsT=sel_bf(h),
                                           rhs=Gall[:, j * 512 + c0:(j + 1) * 512],
                                           start=False, stop=(not band), skip_group_check=True), base, m=(not band))
                if band:
                    tk = PE(lambda e: e.matmul(banks[sbk][:, c0:c0 + 128], lhsT=ident_bf,
                                               rhs=maskf_bf(b_ - 4 * j)[:, c0:c0 + 128],
                                               start=False, stop=True, skip_group_check=True), base, m=True)
                t_qk[i] = tk

            def ex(i):
                j, b_ = steps[i]
                sl = i % NSL
                sbk = zb[i]
                c0 = crange(j, b_)
                t_ex[i] = ACT(lambda e: e.activation(out=a_sl[sl][:, c0:512], in_=banks[sbk][:, c0:512], func=AF.Exp,
                                                     bias=negc_tm[:, b_ * H + h: b_ * H + h + 1], scale=1.0),
                              [t_qk[i], aslot_free[sl]] + base)
                st["bank_free"][sbk] = t_ex[i]

            def pv(i):
                j, b_ = steps[i]
                sl = i % NSL
                oi = oinfo[j]
                ob = banks[4 + oi]
                first = (b_ == 0)
                lastb = (b_ == 4 * j + 3)
                c0 = crange(j, b_)
                PE(lambda e: e.matmul(ob[:, c0:512], lhsT=vsb[:, b_ * 128:(b_ + 1) * 128], rhs=a_sl[sl][:, c0:512],
                                      start=first, stop=lastb, skip_group_check=True),
                   [t_ex[i], o_free[oi] if first else None] + base)
                alt = (j % 2 == 1)
                rsb = tb_bf[0].bitcast(F32)[:, :] if alt else banks[3][:, :]
                rs_free = st["tb_free"][0] if alt else st["bank_free"][3]
                tk = PE(lambda e: e.matmul(rsb[:, c0:512], lhsT=ones_bf, rhs=a_sl[sl][:, c0:512], start=first,
                                           stop=lastb, skip_group_check=True),
                        [t_ex[i], rs_free if first else None] + base, m=True)
                aslot_free[sl] = tk
                if lastb:
                    t_c = DVE(lambda e: e.tensor_copy(out=rinv, in_=rsb[:, :]), [tk])
                    if alt:
                        st["tb_free"][0] = t_c
                    else:
                        st["bank_free"][3] = t_c
                    t_r = DVE(lambda e: e.reciprocal(out=rinv, in_=rinv), [t_c])
                    t_t = DVE(lambda e: e.tensor_tensor(out=ttmp, in0=ob[:, :], in1=rinv, op=ALU.mult), [t_r])
                    o_free[oi] = t_t
                    t_g_last[0] = DVE(lambda e: e.tensor_tensor(
                        out=gT[gs][:, j * 512:(j + 1) * 512], in0=ttmp, in1=szT[:, j * 512:(j + 1) * 512],
                        op=ALU.mult), [t_t, g_free[gs]] + base)

            qk(0)
            if n > 1:
                qk(1)
            for i in range(n):
                ex(i)
                if i + 2 < n:
                    qk(i + 2)
                pv(i)
            return gs, t_g_last[0]

        def emit_attn_sb(l, h, ready, phase_waits, pset=0, gen=None):
            gs = 0
            qT, kT, szT, vsb = psets[pset]
            base = ready + consts_ready + phase_waits
            xb = banks[3]
            steps = []
            for j in range(NQ):
                for b_ in range(4 * j + 3, -1, -1):
                    steps.append((j, b_))
            n = len(steps)
            zb = [None] * n
            t_qk = [None] * n
            t_exp = [None] * n
            t_ln = [None] * n
            t_nt = [None] * n
            t_er = [None] * n
            t_ct = [None] * n
            t_mul = [None] * n
            oinfo = {}
            for j in range(NQ):
                oinfo[j] = 0
            t_g_last = [None]
            x_free = [st["bank_free"][3]]

            eps = [banks[2][:, :], tb_bf[0].bitcast(F32)[:, :], tb_bf[1].bitcast(F32)[:, :]]
            e_free = [st["bank_free"][2], st["tb_free"][0], st["tb_free"][1]]
            NE = 3

            def crange(j, b_):
                i_ = b_ - 4 * j
                return (128 * i_ if i_ > 0 else 0)

            def qk(i):
                j, b_ = steps[i]
                sbk = i % 2
                zb[i] = sbk
                band = b_ >= 4 * j
                c0 = crange(j, b_)
                tk = PE(lambda e: e.matmul(banks[sbk][:, c0:512], lhsT=kT[:, b_ * 128:(b_ + 1) * 128],
                                           rhs=qT[:, j * 512 + c0:(j + 1) * 512], start=True, stop=(not band)),
                        base + [st["bank_free"][sbk]], m=(not band))
                if band:
                    tk = PE(lambda e: e.matmul(banks[sbk][:, c0:c0 + 128], lhsT=ident_bf,
                                               rhs=masks_bf(b_ - 4 * j)[:, c0:c0 + 128],
                                               start=False, stop=True, skip_group_check=True), base, m=True)
                t_qk[i] = tk

            def a_exp(i):
                j, b_ = steps[i]
                sl = i % NE
                sbk = zb[i]
                c0 = crange(j, b_)
                t_exp[i] = ACT(lambda e: e.activation(out=eps[sl][:, c0:512], in_=banks[sbk][:, c0:512], func=AF.Exp),
                               [t_qk[i], e_free[sl]] + base)
                st["bank_free"][sbk] = t_exp[i]

            def a_ln(i):
                j, b_ = steps[i]
                sl = i % NSL
                c0 = crange(j, b_)
                t_ln[i] = ACT(lambda e: e.activation(out=sp_sl[sl][:, c0:512], in_=eps[i % NE][:, c0:512], func=AF.Ln,
                                                     bias=1.0, scale=1.0), [t_exp[i], spslot_free[sl]])

            def p_nt(i):
                j, b_ = steps[i]
                sl = i % NSL
                c0 = crange(j, b_)
                first = (b_ == 4 * j + 3)
                w = [t_ln[i]]
                if i > 0:
                    w.append(t_er[i - 1])
                if first:
                    w.append(x_free[0])
                t_nt[i] = PE(lambda e: e.matmul(xb[:, c0:512], lhsT=ntri_bf, rhs=sp_sl[sl][:, c0:512], start=first,
                                                stop=True, skip_group_check=True), w + base, m=True)

            def a_er(i):
                j, b_ = steps[i]
                sl = i % NSL
                c0 = crange(j, b_)
                t_er[i] = ACT(lambda e: e.activation(out=er_sl[sl][:, c0:512], in_=xb[:, c0:512], func=AF.Exp),
                              [t_nt[i], erslot_free[sl]])

            def p_ct(i):
                j, b_ = steps[i]
                sl = i % NSL
                c0 = crange(j, b_)
                if b_ == 0:
                    spslot_free[sl] = t_nt[i]
                    return
                t_ct[i] = PE(lambda e: e.matmul(xb[:, c0:512], lhsT=ctri_bf, rhs=sp_sl[sl][:, c0:512], start=False,
                                                stop=True, skip_group_check=True), [t_er[i]], m=True)
                spslot_free[sl] = t_ct[i]

            def d_mul(i):
                j, b_ = steps[i]
                sl = i % NSL
                c0 = crange(j, b_)
                t_mul[i] = DVE(lambda e: e.tensor_tensor(out=a_sl[sl][:, c0:512], in0=eps[i % NE][:, c0:512],
                                                         in1=er_sl[sl][:, c0:512], op=ALU.mult),
                               [t_er[i], aslot_free[sl]])
                e_free[i % NE] = t_mul[i]
                erslot_free[sl] = t_mul[i]

            def p_pv(i):
                j, b_ = steps[i]
                sl = i % NSL
                c0 = crange(j, b_)
                oi = oinfo[j]
                ob = banks[4 + oi]
                first = (b_ == 4 * j + 3)
                lastb = (b_ == 0)
                tk = PE(lambda e: e.matmul(ob[:, c0:512], lhsT=vsb[:, b_ * 128:(b_ + 1) * 128], rhs=a_sl[sl][:, c0:512],
                                           start=first, stop=lastb, skip_group_check=True),
                        [t_mul[i], o_free[oi] if first else None] + base, m=True)
                aslot_free[sl] = tk
                if lastb:
                    t_g = DVE(lambda e: e.tensor_tensor(out=gT[gs][:, j * 512:(j + 1) * 512], in0=ob[:, :],
                                                        in1=szT[:, j * 512:(j + 1) * 512], op=ALU.mult),
                              [tk, g_free[gs]] + base)
                    o_free[oi] = t_g
                    t_g_last[0] = t_g

            qk(0)
            if n > 1:
                qk(1)
            a_exp(0)
            if n > 1:
                a_exp(1)
            a_ln(0)
            p_nt(0)
            for i in range(n):
                if i + 2 < n:
                    qk(i + 2)
                a_er(i)
                if i + 1 < n:
                    a_ln(i + 1)
                if i + 2 < n:
                    a_exp(i + 2)
                p_ct(i)
                if i + 1 < n:
                    p_nt(i + 1)
                d_mul(i)
                if i >= 1:
                    p_pv(i - 1)
                if gen is not None:
                    next(gen, None)
            p_pv(n - 1)
            if gen is not None:
                for _ in gen:
                    pass
            st["bank_free"][3] = t_er[n - 1]
            st["bank_free"][2] = t_mul[n - 1]
            st["tb_free"][0] = t_mul[n - 1]
            st["tb_free"][1] = t_mul[n - 1]
            return gs, t_g_last[0]

        cstate = {"xb_free": [None, None], "xnb_free": [None, None], "gb_free": [None, None],
                  "r_free": [None, None], "xnew_free": [[], []], "y_free": [None, None],
                  "wost_free": [None, None]}
        out_tickets = []

        def emit_phase_c(l, phase_waits):
            lastl = (l == DEPTH - 1)
            src = x_d if l == 0 else xres_d
            dst = out_d if lastl else xres_d
            wcast = []
            for h in range(H):
                i = h % 2
                t_ld = DMA(lambda e, h=h, i=i: e.dma_start(out=wost[i], in_=wo_d[l, :, h * D:(h + 1) * D]),
                           s_dwo[i], [cstate["wost_free"][i]] + phase_waits)
                t_c = POOL(lambda e, h=h, i=i: e.tensor_copy(out=wobf[:, h * D:(h + 1) * D], in_=wost[i]),
                           [t_ld] + phase_waits)
                cstate["wost_free"][i] = t_c
                wcast.append(t_c)
            t_g = DMA(lambda e: e.dma_start(out=gB, in_=lng_d[l]), s_lng, phase_waits)
            t_b = DMA(lambda e: e.dma_start(out=bB, in_=lnb_d[l]), s_lnb, phase_waits)
            nch = max(1, D // 512)
            csz = D // nch
            new_xT = [None] * NT
            loads = {}
            SW = nch * 6 + 8

            def issue_loads(tb):
                i = tb % 2
                t_x = DMA(lambda e: e.dma_start(out=xblk[i], in_=src[tb * 128:(tb + 1) * 128, :]),
                          s_dx[i], [cstate["xb_free"][i]] + phase_waits)
                t_gl = DMA(lambda e: e.dma_start(
                    out=gblk[i].rearrange("p (h c) -> p h c", c=128),
                    in_=gscr_d[:, :, tb * 128:(tb + 1) * 128].rearrange("h p c -> p h c")),
                    s_dg[i], [cstate["gb_free"][i]] + phase_waits)
                loads[tb] = (t_x, t_gl)

            def sviews(i):
                stv = stats[:, i * SW: i * SW + nch * 6]
                mv = stats[:, i * SW + nch * 6: i * SW + nch * 6 + 2]
                lnv = stats[:, i * SW + nch * 6 + 2: i * SW + nch * 6 + 3]
                rstd = stats[:, i * SW + nch * 6 + 3: i * SW + nch * 6 + 4]
                return stv, mv, lnv, rstd

            tkA = {}
            tB = {}
            tC = {}
            stat_free = cstate.setdefault("stat_free", [None, None])

            def stA(tb):
                i = tb % 2
                t_x, t_gl = loads[tb]
                gv = gblk[i].rearrange("p (h c) -> p h c", c=128)
                ybanks = [banks[2 * i + k] for k in range(nch)]
                tk = None
                for k in range(nch):
                    for h in range(H):
                        tk = PE(lambda e, k=k, h=h: e.matmul(
                            ybanks[k][:, :csz], lhsT=gv[:, h, :], rhs=wobf[:, h * D + k * csz: h * D + (k + 1) * csz],
                            start=(h == 0), stop=(h == H - 1)),
                            [t_gl, cstate["y_free"][i]] + wcast + phase_waits, m=(h == H - 1 and k == nch - 1))
                cstate["gb_free"][i] = tk
                tkA[tb] = tk

            def stB(tb):
                i = tb % 2
                t_x, t_gl = loads[tb]
                ybanks = [banks[2 * i + k] for k in range(nch)]
                stv, mv, lnv, rstd = sviews(i)
                t_r = None
                for k in range(nch):
                    t_r = DVE(lambda e, k=k: e.scalar_tensor_tensor(
                        out=rbuf[i][:, k * csz:(k + 1) * csz], in0=xblk[i][:, k * csz:(k + 1) * csz],
                        scalar=float(cfg.alpha), in1=ybanks[k][:, :csz], op0=ALU.mult, op1=ALU.add),
                        [tkA[tb], t_x, cstate["r_free"][i]] + phase_waits)
                cstate["xb_free"][i] = t_r
                cstate["y_free"][i] = t_r
                t_s = None
                for k in range(nch):
                    t_s = DVE(lambda e, k=k: e.bn_stats(out=stv[:, k * 6:(k + 1) * 6],
                                                        in_=rbuf[i][:, k * csz:(k + 1) * csz]), [t_r])
                t_a = DVE(lambda e: e.bn_aggr(out=mv, in_=stv), [t_s])
                t_l1 = ACT(lambda e: e.activation(out=lnv, in_=mv[:, 1:2], func=AF.Ln, bias=float(LN_EPS), scale=1.0),
                           [t_a, stat_free[i]])
                t_l2 = ACT(lambda e: e.activation(out=rstd, in_=lnv, func=AF.Exp, scale=-0.5), [t_l1])
                tB[tb] = (t_a, t_l2)

            def stC(tb):
                i = tb % 2
                stv, mv, lnv, rstd = sviews(i)
                t_a, t_l2 = tB[tb]
                t_n = DVE(lambda e: e.tensor_scalar(
                    out=rbuf[i], in0=rbuf[i], scalar1=mv[:, 0:1], scalar2=rstd, op0=ALU.subtract, op1=ALU.mult),
                    [t_l2, t_a])
                stat_free[i] = t_n
                t_m = POOL(lambda e: e.tensor_tensor(out=rbuf[i], in0=rbuf[i], in1=gB, op=ALU.mult),
                           [t_n, t_g] + phase_waits)
                t_o = POOL(lambda e: e.tensor_tensor(out=xnew[i], in0=rbuf[i], in1=bB, op=ALU.add),
                           [t_m, t_b] + cstate["xnew_free"][i] + phase_waits)
                cstate["r_free"][i] = t_o
                t_st = DMA(lambda e: e.dma_start(out=dst[tb * 128:(tb + 1) * 128, :], in_=xnew[i]), s_out[i], [t_o])
                out_tickets.append(t_st)
                if not lastl:
                    t_cast = ACT(lambda e: e.activation(out=xnb[i], in_=xnew[i], func=AF.Copy),
                                 [t_o, cstate["xnb_free"][i]])
                    cstate["xnew_free"][i] = [t_st, t_cast]
                    tC[tb] = t_cast
                else:
                    cstate["xnew_free"][i] = [t_st]

            def stD(tb):
                if lastl:
                    return
                i = tb % 2
                tkt = None
                for kc in range(KD):
                    tkt = PE(lambda e, kc=kc: e.transpose(out=tb_bf[i][:, kc * 128:(kc + 1) * 128],
                                                          in_=xnb[i][:, kc * 128:(kc + 1) * 128], identity=ident_bf),
                             [tC[tb], st["tb_free"][i]] + consts_ready + phase_waits, m=(kc == KD - 1))
                cstate["xnb_free"][i] = tkt
                dstx = xT[:, :, tb * 128:(tb + 1) * 128]
                srcx = tb_bf[i][:, :KD * 128].rearrange("p (k c) -> p k c", c=128)
                t_cp = DVE(lambda e: e.tensor_copy(out=dstx, in_=srcx), [tkt])
                st["tb_free"][i] = t_cp
                st["xT_ready"][tb] = t_cp
                new_xT[tb] = t_cp

            issue_loads(0)
            if NT > 1:
                issue_loads(1)
            stA(0)
            for t in range(NT + 2):
                if t + 1 < NT:
                    stA(t + 1)
                if t < NT:
                    stB(t)
                    if t + 2 < NT:
                        issue_loads(t + 2)
                if 0 <= t - 1 < NT:
                    stC(t - 1)
                if 0 <= t - 2 < NT:
                    stD(t - 2)
            return new_xT

        phase_waits = []
        phase_waits = [t for t in all_xT[-2:]] + consts_ready
        emit_wload(0, 0)
        gstore = [None]

        def nxt_of(l, h):
            return (l, h + 1) if h + 1 < H else ((l + 1, 0) if l + 1 < DEPTH else None)

        def run_all(g):
            for _ in g:
                pass

        for l in range(DEPTH):
            is_fox = (cfg.mixers[l] == 'fox')
            pw = list(phase_waits)
            fox_ready = []
            set_done = [[], []]
            if is_fox:
                fox_ready = emit_fox_pre(l, pw)
                pw = pw + fox_ready
                for h in range(H):
                    emit_wcast(l, h)
                    nxt = nxt_of(l, h)
                    if nxt is not None:
                        emit_wload(nxt[0], nxt[1], extra_waits=wl["cast"][(l, h)])
                    outd = {}
                    run_all(proj_gen(l, h, 0, pw + set_done[0], False, outd))
                    gs, t_g = emit_attn_fox(l, h, outd["ready"], fox_ready, pw, pset=0)
                    set_done[0] = [t_g, (s_pe, s_pe.v)]
                    t_store = DMA(lambda e, h=h: e.dma_start(out=gscr_d[h], in_=gT[0]), s_gst[0], [t_g])
                    g_free[0] = t_store
                    gstore[0] = t_store
            else:
                emit_wcast(l, 0)
                nxt = nxt_of(l, 0)
                if nxt is not None:
                    emit_wload(nxt[0], nxt[1], extra_waits=wl["cast"][(l, 0)])
                outd = {}
                run_all(proj_gen(l, 0, 0, pw, False, outd))
                ready = outd["ready"]
                for h in range(H):
                    gen = None
                    outn = {}
                    if h + 1 < H:
                        emit_wcast(l, h + 1, pool_only=True)
                        nxt = nxt_of(l, h + 1)
                        if nxt is not None:
                            emit_wload(nxt[0], nxt[1], extra_waits=wl["cast"][(l, h + 1)])
                        gen = proj_gen(l, h + 1, (h + 1) % 2, pw + set_done[(h + 1) % 2], True, outn)
                    gs, t_g = emit_attn_sb(l, h, ready, pw, pset=h % 2, gen=gen)
                    set_done[h % 2] = [t_g, (s_pe, s_pe.v)]
                    t_store = DMA(lambda e, h=h: e.dma_start(out=gscr_d[h], in_=gT[0]), s_gst[0], [t_g])
                    g_free[0] = t_store
                    gstore[0] = t_store
                    ready = outn.get("ready")
            st["attn_done"] = set_done[0] + set_done[1]
            pwc = [t for t in gstore if t is not None] + st["attn_done"] + [(s_act, s_act.v), (s_dve, s_dve.v)]
            new_xT = emit_phase_c(l, pwc)
            if l + 1 < DEPTH:
                all_xT[:] = new_xT
                phase_waits = [(s_pe, s_pe.v), (s_act, s_act.v), (s_dve, s_dve.v), (s_pool, s_pool.v)] + \
                    out_tickets[-2:] + consts_ready
                st["attn_done"] = []
        R.op("sync", None, [(s_out[0], s_out[0].v), (s_out[1], s_out[1].v)])
        if DBG:
            R.op("gpsimd", None, [(s_dbg, s_dbg.v)])

        with nc.Block() as block:
            @block.sync
            def _(e):
                R.replay("sync", e)

            @block.tensor
            def _(e):
                R.replay("tensor", e)

            @block.vector
            def _(e):
                R.replay("vector", e)

            @block.scalar
            def _(e):
                R.replay("scalar", e)

            @block.gpsimd
            def _(e):
                R.replay("gpsimd", e)
    return nc


def prepare_inputs(cfg, x, fox_w_in, fox_b_f, fox_w_out, sb_w_in, sb_w_out, ln_g, ln_b):
    w_in_layers, w_out_layers = [], []
    for l in range(cfg.DEPTH):
        slot = l // 2
        if l % 2 == 0:
            w_in_layers.append(np.asarray(fox_w_in[slot]))
            w_out_layers.append(np.asarray(fox_w_out[slot]))
        else:
            w_in_layers.append(np.asarray(sb_w_in[slot]))
            w_out_layers.append(np.asarray(sb_w_out[slot]))
    wq, wo = layout_weights(cfg, w_in_layers, w_out_layers)
    wf = layout_wf(cfg, np.asarray(fox_w_in))
    bf = np.ascontiguousarray(np.asarray(fox_b_f, np.float32).reshape(cfg.NFOX, cfg.H, 1))
    lng = np.ascontiguousarray(np.broadcast_to(np.asarray(ln_g, np.float32)[:, None, :], (cfg.DEPTH, 128, cfg.D)))
    lnb = np.ascontiguousarray(np.broadcast_to(np.asarray(ln_b, np.float32)[:, None, :], (cfg.DEPTH, 128, cfg.D)))
    cb, cf = make_consts(cfg)
    shared = {"wq": wq, "wo": wo, "wf": wf, "bf": bf, "lng": lng, "lnb": lnb, "cb": cb, "cf": cf}
    in_maps = []
    for b in range(x.shape[0]):
        m = dict(shared)
        m["x"] = np.ascontiguousarray(np.asarray(x[b], np.float32))
        in_maps.append(m)
    return in_maps


_CACHE = {}


def kernel(x, fox_w_in, fox_b_f, fox_w_out, sb_w_in, sb_w_out, ln_g, ln_b):
    cfg = Cfg()
    x = np.asarray(x)
    in_maps = prepare_inputs(cfg, x, fox_w_in, fox_b_f, fox_w_out, sb_w_in, sb_w_out, ln_g, ln_b)
    if "nc" not in _CACHE:
        _CACHE["nc"] = build_program(cfg)
    nc = _CACHE["nc"]
    res = run_bass_kernel_spmd(nc, in_maps, core_ids=list(range(8)))
    out = np.stack([np.asarray(r["out"]) for r in res.results], axis=0)
    return out.astype(np.float32)
```
